# Optimizing a Trainium2 kernel written in Bass

```python
import math, functools
import jax, jax.numpy as jnp
from jax import lax
import numpy as np

D_MODEL = 1024
BATCH = 32
SEQ = 256
DEPTH = 4
DEC_BATCH = 4
DEC_SEQ = 2048
PAST_LEN = 256

GRID_W = 64
N_MIXERS = 3
N_SSD_LAYERS = (DEPTH + 2) // 3
N_CONV_LAYERS = (DEPTH + 1) // 3
N_DIFF_LAYERS = DEPTH // 3
SSM_D_INNER = 2 * D_MODEL
SSM_HEAD_DIM = 64
SSM_HEADS = SSM_D_INNER // SSM_HEAD_DIM
SSM_GROUPS = 8
SSM_STATE = 128
SSM_CONV = 3
SSM_CHUNK = 128
SSM_CONV_DIM = SSM_D_INNER + 2 * SSM_GROUPS * SSM_STATE
SSM_IN_DIM = SSM_D_INNER + SSM_CONV_DIM + 2 * SSM_HEADS
SHORT_CONV = 3
DIFF_HEAD_DIM = 64
DIFF_HEADS = D_MODEL // (2 * DIFF_HEAD_DIM)
Q_BLOCK = 128
ROPE_THETA = 10000.0
ROT_PAIRS_PER_AXIS = DIFF_HEAD_DIM // 4
FFN_DIM = 2816
FFN_CONV = 3
NORM_EPS = 1e-6

kernel_name = 'hybrid_diffusion_ssd_conv_diffattn_step'


def rmsnorm(x, g):
    xf = x.astype(jnp.float32)
    y = xf * lax.rsqrt(jnp.mean(xf * xf, axis=-1, keepdims=True) + NORM_EPS)
    return (y * g.astype(jnp.float32)).astype(x.dtype)


def modulation(cond, w, b):
    return jax.nn.silu(cond) @ w + b


def sandwich_layer(x, mod, g, mix_fn, ffn_fn):
    sh1, sc1, gt1, sh2, sc2, gt2 = jnp.split(mod, 6, axis=-1)
    m, aux = mix_fn(rmsnorm(x, g[0]) * (1 + sc1) + sh1)
    x = x + gt1 * rmsnorm(m, g[1])
    f = ffn_fn(rmsnorm(x, g[2]) * (1 + sc2) + sh2)
    x = x + gt2 * rmsnorm(f, g[3])
    return x, aux


def dwconv(x, w):
    k = w.shape[0]
    pad = k // 2
    L = x.shape[1]
    xp = jnp.pad(x, ((0, 0), (pad, pad), (0, 0)))
    out = xp[:, 0:L] * w[0]
    for i in range(1, k):
        out = out + xp[:, i:i + L] * w[i]
    return out


def conv_ffn(h, w_up, conv_w, w_down):
    u = dwconv(h @ w_up, conv_w)
    gate, val = jnp.split(u, 2, axis=-1)
    return (jax.nn.silu(gate) * val) @ w_down


def short_conv_mixer(h, w_in, conv_w, w_out):
    bg, cg, u = jnp.split(h @ w_in, 3, axis=-1)
    return (bg * dwconv(cg * u, conv_w)) @ w_out, None


def ssd_scan(x, dt, a, b_in, c_in, h0):
    bsz, L, H, P = x.shape
    G, N = b_in.shape[2], b_in.shape[3]
    R = H // G
    nc = L // SSM_CHUNK
    f32 = jnp.float32
    xf = x.astype(f32).reshape(bsz, nc, SSM_CHUNK, G, R, P)
    dtc = dt.reshape(bsz, nc, SSM_CHUNK, G, R)
    bc = b_in.astype(f32).reshape(bsz, nc, SSM_CHUNK, G, N)
    cc = c_in.astype(f32).reshape(bsz, nc, SSM_CHUNK, G, N)
    acs = jnp.cumsum(dtc * a.reshape(G, R), axis=2)
    mask = jnp.tril(jnp.ones((SSM_CHUNK, SSM_CHUNK), bool))[:, :, None, None]
    decay = jnp.exp(jnp.where(mask, acs[:, :, :, None] - acs[:, :, None], -jnp.inf))
    cb = jnp.einsum('bcqgn,bcsgn->bcqsg', cc, bc)
    y_diag = jnp.einsum('bcqsgr,bcsgrp->bcqgrp', cb[..., None] * decay * dtc[:, :, None], xf)
    xw = xf * (jnp.exp(acs[:, :, -1:] - acs) * dtc)[..., None]
    chunk_states = jnp.einsum('bcsgn,bcsgrp->bcgrpn', bc, xw)
    chunk_decay = jnp.exp(acs[:, :, -1])

    def step(h, inp):
        st, dec = inp
        return h * dec[..., None, None] + st, h

    h_last, h_in = lax.scan(step, h0.astype(f32).reshape(bsz, G, R, P, N),
                            (jnp.swapaxes(chunk_states, 0, 1), jnp.swapaxes(chunk_decay, 0, 1)))
    h_in = jnp.swapaxes(h_in, 0, 1)
    y_off = jnp.einsum('bcqgn,bcgrpn->bcqgrp', cc, h_in) * jnp.exp(acs)[..., None]
    return (y_diag + y_off).reshape(bsz, L, H, P), h_last.reshape(bsz, H, P, N)


def ssd_mixer(h, h0, w_in, conv_w, conv_b, dt_bias, a_log, d_skip, norm_g, w_out):
    bsz, L, _ = h.shape
    z, xbc, dt_raw = jnp.split(h @ w_in, [SSM_D_INNER, SSM_D_INNER + SSM_CONV_DIM], axis=-1)
    xbc = jax.nn.silu(dwconv(xbc, conv_w) + conv_b)
    xs, bm, cm = jnp.split(xbc, [SSM_D_INNER, SSM_D_INNER + SSM_GROUPS * SSM_STATE], axis=-1)
    xs = xs.reshape(bsz, L, SSM_HEADS, SSM_HEAD_DIM)
    bm = bm.reshape(bsz, L, SSM_GROUPS, SSM_STATE)
    cm = cm.reshape(bsz, L, SSM_GROUPS, SSM_STATE)
    dt = jax.nn.softplus(dt_raw.astype(jnp.float32) + dt_bias.reshape(-1).astype(jnp.float32))
    dt_f, dt_b = jnp.split(dt, 2, axis=-1)
    a = -jnp.exp(a_log.astype(jnp.float32))
    y_f, s_f = ssd_scan(xs, dt_f, a[0], bm, cm, h0[:, 0])
    flip = lambda t: jnp.flip(t, axis=1)
    y_b, s_b = ssd_scan(flip(xs), flip(dt_b), a[1], flip(bm), flip(cm), h0[:, 1])
    y = y_f + flip(y_b) + d_skip.astype(jnp.float32)[:, None] * xs.astype(jnp.float32)
    y = rmsnorm(y.reshape(bsz, L, SSM_D_INNER) * jax.nn.silu(z.astype(jnp.float32)), norm_g)
    return y.astype(h.dtype) @ w_out, jnp.stack([s_f, s_b], axis=1)


def axial_rotary(n_tokens):
    rows = n_tokens // GRID_W
    row = jnp.repeat(jnp.arange(rows, dtype=jnp.float32), GRID_W)
    col = jnp.tile(jnp.arange(GRID_W, dtype=jnp.float32), rows)
    inv = ROPE_THETA ** (-jnp.arange(ROT_PAIRS_PER_AXIS, dtype=jnp.float32) / ROT_PAIRS_PER_AXIS)
    ang = jnp.concatenate([row[:, None] * inv, col[:, None] * inv], axis=-1)
    return jnp.cos(ang), jnp.sin(ang)


def apply_rotary(x, cos, sin):
    half = x.shape[-1] // 2
    x1, x2 = x[..., :half], x[..., half:]
    c = cos[None, :, None].astype(x.dtype)
    s = sin[None, :, None].astype(x.dtype)
    return jnp.concatenate([x1 * c - x2 * s, x2 * c + x1 * s], axis=-1)


def diff_qkv(h, w_qkv):
    bsz, L, _ = h.shape
    q, k, v = jnp.split(h @ w_qkv, 3, axis=-1)
    return (q.reshape(bsz, L, 2 * DIFF_HEADS, DIFF_HEAD_DIM),
            k.reshape(bsz, L, 2 * DIFF_HEADS, DIFF_HEAD_DIM),
            v.reshape(bsz, L, DIFF_HEADS, 2 * DIFF_HEAD_DIM))


def diff_lambda(lp, lam_init):
    lp = lp.astype(jnp.float32)
    return jnp.exp(jnp.sum(lp[0] * lp[1])) - jnp.exp(jnp.sum(lp[2] * lp[3])) + lam_init


def block_diff_attention(q, k, v, lam):
    bsz, Lq, H2, d = q.shape
    nb = Lq // Q_BLOCK
    qb = jnp.swapaxes(q.reshape(bsz, nb, Q_BLOCK, H2, d), 0, 1)
    scale = d ** -0.5

    def one_block(qi):
        s = jnp.einsum('bqhd,bkhd->bhqk', qi, k).astype(jnp.float32) * scale
        p = jax.nn.softmax(s, axis=-1).reshape(bsz, H2 // 2, 2, Q_BLOCK, -1)
        att = p[:, :, 0] - lam * p[:, :, 1]
        return jnp.einsum('bhqk,bkhe->bqhe', att.astype(v.dtype), v)

    o = lax.map(one_block, qb)
    return jnp.swapaxes(o, 0, 1).reshape(bsz, Lq, H2 // 2, 2 * d)


def diff_out(o, lam_init, subln_g, w_out):
    bsz, L = o.shape[0], o.shape[1]
    o = rmsnorm(o, subln_g) * (1.0 - lam_init)
    return o.reshape(bsz, L, D_MODEL) @ w_out


def setup_inputs(seed: int = 0) -> dict:
    key = jax.random.key(seed)
    ks = iter(jax.random.split(key, 40))
    f32 = jnp.float32
    D = D_MODEL

    def nrm(shape, scale):
        return jax.random.normal(next(ks), shape, f32) * scale

    def gain(shape):
        return 1.0 + nrm(shape, 0.05)

    x_prompt = nrm((BATCH, SEQ, D), 1.0)
    x_sample = nrm((DEC_BATCH, DEC_SEQ, D), 1.0)
    state_ssm = nrm((DEC_BATCH, N_SSD_LAYERS, 2, SSM_HEADS, SSM_HEAD_DIM, SSM_STATE), 0.3)
    cache_k = nrm((DEC_BATCH, N_DIFF_LAYERS, PAST_LEN, 2 * DIFF_HEADS, DIFF_HEAD_DIM), 1.0)
    cache_v = nrm((DEC_BATCH, N_DIFF_LAYERS, PAST_LEN, DIFF_HEADS, 2 * DIFF_HEAD_DIM), 1.0)
    c = nrm((DEC_BATCH, D), 1.0)
    c_ctx = nrm((D,), 1.0)
    w_mod = nrm((DEPTH, D, 6 * D), 0.5 * D ** -0.5)
    b_mod = nrm((DEPTH, 6 * D), 0.02)
    norm_g = gain((DEPTH, 4, D))
    ssd_w_in = nrm((N_SSD_LAYERS, D, SSM_IN_DIM), D ** -0.5)
    ssd_conv_w = nrm((N_SSD_LAYERS, SSM_CONV, SSM_CONV_DIM), SSM_CONV ** -0.5)
    ssd_conv_b = nrm((N_SSD_LAYERS, SSM_CONV_DIM), 0.02)
    dt0 = jnp.exp(jax.random.uniform(next(ks), (N_SSD_LAYERS, 2, SSM_HEADS), f32,
                                     math.log(1e-3), math.log(1e-1)))
    ssd_dt_bias = dt0 + jnp.log(-jnp.expm1(-dt0))
    ssd_a_log = jnp.log(jax.random.uniform(next(ks), (N_SSD_LAYERS, 2, SSM_HEADS), f32, 1.0, 16.0))
    ssd_d = 1.0 + nrm((N_SSD_LAYERS, SSM_HEADS), 0.1)
    ssd_norm_g = gain((N_SSD_LAYERS, SSM_D_INNER))
    ssd_w_out = nrm((N_SSD_LAYERS, SSM_D_INNER, D), SSM_D_INNER ** -0.5)
    sc_w_in = nrm((N_CONV_LAYERS, D, 3 * D), D ** -0.5)
    sc_conv_w = nrm((N_CONV_LAYERS, SHORT_CONV, D), SHORT_CONV ** -0.5)
    sc_w_out = nrm((N_CONV_LAYERS, D, D), D ** -0.5)
    da_w_qkv = nrm((N_DIFF_LAYERS, D, 3 * D), D ** -0.5)
    da_lambda = nrm((N_DIFF_LAYERS, 4, DIFF_HEAD_DIM), 0.1)
    da_subln_g = gain((N_DIFF_LAYERS, 2 * DIFF_HEAD_DIM))
    da_w_out = nrm((N_DIFF_LAYERS, D, D), D ** -0.5)
    ffn_w_up = nrm((DEPTH, D, 2 * FFN_DIM), D ** -0.5)
    ffn_conv_w = nrm((DEPTH, FFN_CONV, 2 * FFN_DIM), FFN_CONV ** -0.5)
    ffn_w_down = nrm((DEPTH, FFN_DIM, D), FFN_DIM ** -0.5)
    return {'x_prompt': x_prompt, 'x_sample': x_sample, 'state_ssm': state_ssm,
            'cache_k': cache_k, 'cache_v': cache_v, 'c': c, 'c_ctx': c_ctx,
            'w_mod': w_mod, 'b_mod': b_mod, 'norm_g': norm_g,
            'ssd_w_in': ssd_w_in, 'ssd_conv_w': ssd_conv_w, 'ssd_conv_b': ssd_conv_b,
            'ssd_dt_bias': ssd_dt_bias, 'ssd_a_log': ssd_a_log, 'ssd_d': ssd_d,
            'ssd_norm_g': ssd_norm_g, 'ssd_w_out': ssd_w_out,
            'sc_w_in': sc_w_in, 'sc_conv_w': sc_conv_w, 'sc_w_out': sc_w_out,
            'da_w_qkv': da_w_qkv, 'da_lambda': da_lambda, 'da_subln_g': da_subln_g,
            'da_w_out': da_w_out, 'ffn_w_up': ffn_w_up, 'ffn_conv_w': ffn_conv_w,
            'ffn_w_down': ffn_w_down}


def reference(x_prompt, x_sample, state_ssm, cache_k, cache_v, c, c_ctx, w_mod, b_mod, norm_g,
              ssd_w_in, ssd_conv_w, ssd_conv_b, ssd_dt_bias, ssd_a_log, ssd_d, ssd_norm_g, ssd_w_out,
              sc_w_in, sc_conv_w, sc_w_out, da_w_qkv, da_lambda, da_subln_g, da_w_out,
              ffn_w_up, ffn_conv_w, ffn_w_down):
    yp, ys = x_prompt, x_sample
    bp = x_prompt.shape[0]
    cos, sin = axial_rotary(x_sample.shape[1])
    new_ssm, new_k, new_v = [], [], []
    for l in range(DEPTH):
        j = l // N_MIXERS
        kind = l % N_MIXERS
        mod_p = modulation(c_ctx, w_mod[l], b_mod[l])[None, None]
        mod_s = modulation(c, w_mod[l], b_mod[l])[:, None]
        ffn = functools.partial(conv_ffn, w_up=ffn_w_up[l], conv_w=ffn_conv_w[l], w_down=ffn_w_down[l])
        if kind == 0:
            ssd_args = (ssd_w_in[j], ssd_conv_w[j], ssd_conv_b[j], ssd_dt_bias[j], ssd_a_log[j],
                        ssd_d[j], ssd_norm_g[j], ssd_w_out[j])
            zeros = jnp.zeros((bp, 2, SSM_HEADS, SSM_HEAD_DIM, SSM_STATE), jnp.float32)
            mix_p = lambda h: ssd_mixer(h, zeros, *ssd_args)
            mix_s = lambda h: ssd_mixer(h, state_ssm[:, j], *ssd_args)
        elif kind == 1:
            mix_p = functools.partial(short_conv_mixer, w_in=sc_w_in[j], conv_w=sc_conv_w[j], w_out=sc_w_out[j])
            mix_s = mix_p
        else:
            lam_init = 0.8 - 0.6 * math.exp(-0.3 * l)
            lam = diff_lambda(da_lambda[j], lam_init)
            w_qkv, g_sub, w_o = da_w_qkv[j], da_subln_g[j], da_w_out[j]

            def mix_p(h):
                q, k, v = diff_qkv(h, w_qkv)
                return diff_out(block_diff_attention(q, k, v, lam), lam_init, g_sub, w_o), (k, v)

            def mix_s(h):
                q, k, v = diff_qkv(h, w_qkv)
                q = apply_rotary(q, cos, sin)
                k = apply_rotary(k, cos, sin)
                k_all = jnp.concatenate([cache_k[:, j].astype(k.dtype), k], axis=1)
                v_all = jnp.concatenate([cache_v[:, j].astype(v.dtype), v], axis=1)
                return diff_out(block_diff_attention(q, k_all, v_all, lam), lam_init, g_sub, w_o), None

        yp, aux = sandwich_layer(yp, mod_p, norm_g[l], mix_p, ffn)
        ys, _ = sandwich_layer(ys, mod_s, norm_g[l], mix_s, ffn)
        if kind == 0:
            new_ssm.append(aux.astype(x_prompt.dtype))
        elif kind == 2:
            new_k.append(aux[0])
            new_v.append(aux[1])
    new_state_ssm = jnp.stack(new_ssm, axis=1)
    new_cache_k = jnp.stack(new_k, axis=1)
    new_cache_v = jnp.stack(new_v, axis=1)
    return (yp, ys, new_state_ssm, new_cache_k, new_cache_v)
```

```python
import contextlib
import numpy as np
import concourse.bass as bass
import concourse.mybir as mybir
from concourse.bass_utils import run_bass_kernel_spmd

F32 = mybir.dt.float32
BF16 = mybir.dt.bfloat16
AF = mybir.ActivationFunctionType
ALU = mybir.AluOpType

T = 2048
D = 1024
KC = 8
NSEG = 8
SEGL = 256
DEPTH = 4
FFN = 2816
NPAIR = 22
EPS = 1e-6
NEG = -30000.0

ENGS = ("pe", "act", "dve", "pool", "sp")


class _Op:
    __slots__ = ("eng", "fn", "deps", "dma", "idx", "gidx", "inc", "cnt", "waits", "dsem", "dcnt", "dprev")

    def __init__(self, eng, fn, dma):
        self.eng = eng
        self.fn = fn
        self.dma = dma
        self.deps = set()
        self.inc = False
        self.cnt = 0
        self.waits = []
        self.dsem = None
        self.dcnt = 0
        self.dprev = None


class Sched:
    def __init__(self, nc, n_dma_sems=32, same_eng_dist=3):
        self.nc = nc
        self.ops = []
        self.per = {e: [] for e in ENGS}
        self.last_w = {}
        self.readers = {}
        self.region_deps = {}
        self.n_dma_sems = n_dma_sems
        self.same_eng_dist = same_eng_dist
        self.out_dmas = []

    def new_phase(self, region):
        s = set(self.region_deps.get(region, ()))
        for tab in (self.last_w, self.readers):
            for k in [k for k in tab if k[0] == region]:
                v = tab.pop(k)
                if isinstance(v, list):
                    s.update(v)
                else:
                    s.add(v)
        self.region_deps[region] = s

    @staticmethod
    def _freeze(fn):
        import types
        if fn.__closure__ is None:
            return fn
        cells = tuple(types.CellType(c.cell_contents) for c in fn.__closure__)
        g = types.FunctionType(fn.__code__, fn.__globals__, fn.__name__, fn.__defaults__, cells)
        g.__kwdefaults__ = fn.__kwdefaults__
        return g

    def op(self, eng, fn, reads=(), writes=(), dma=False, is_out=False):
        fn = self._freeze(fn)
        o = _Op(eng, fn, dma)
        o.idx = len(self.per[eng])
        o.gidx = len(self.ops)
        for k in list(reads) + list(writes):
            assert isinstance(k, tuple), k
            if k[0] in self.region_deps and k not in self.last_w and k not in self.readers:
                o.deps.update(self.region_deps[k[0]])
        for k in reads:
            w = self.last_w.get(k)
            if w is not None:
                o.deps.add(w)
        for k in writes:
            w = self.last_w.get(k)
            if w is not None:
                o.deps.add(w)
            for r in self.readers.get(k, ()):
                o.deps.add(r)
        o.deps.discard(o)
        for k in reads:
            self.readers.setdefault(k, []).append(o)
        for k in writes:
            self.last_w[k] = o
            self.readers[k] = []
        self.per[eng].append(o)
        self.ops.append(o)
        if is_out:
            self.out_dmas.append(o)
        return o

    def _skip(self, o, d):
        if d.eng != o.eng:
            return False
        if d.eng == "pe":
            return True
        return (o.idx - d.idx) >= self.same_eng_dist

    def emit(self):
        nc = self.nc
        for o in self.ops:
            best = {}
            for d in o.deps:
                if d.dma or self._skip(o, d):
                    continue
                if d.eng not in best or best[d.eng].idx < d.idx:
                    best[d.eng] = d
            for d in best.values():
                d.inc = True
            o.deps = set(d for d in o.deps if d.dma) | set(best.values())
        fin = _Op("sp", lambda e: e.nop(), False)
        fin.idx = len(self.per["sp"])
        fin.gidx = len(self.ops)
        for o in self.out_dmas:
            fin.deps.add(o)
        for e in ENGS:
            if e != "sp" and self.per[e]:
                last = self.per[e][-1]
                if not last.dma:
                    last.inc = True
                fin.deps.add(last)
        self.per["sp"].append(fin)
        self.ops.append(fin)
        cnt = {e: 0 for e in ENGS}
        dma_n = [0] * self.n_dma_sems
        dma_last = [None] * self.n_dma_sems
        rr = 0
        for o in self.ops:
            if o.dma:
                j = rr % self.n_dma_sems
                rr += 1
                o.dsem = j
                dma_n[j] += 16
                o.dcnt = dma_n[j]
                o.dprev = dma_last[j]
                dma_last[j] = o
            elif o.inc:
                cnt[o.eng] += 1
                o.cnt = cnt[o.eng]
        known = {e: {} for e in ENGS}
        snap = {}
        for o in self.ops:
            kn = known[o.eng]
            need = {}
            deps = list(o.deps)
            if o.dma and o.dprev is not None:
                deps.append(o.dprev)
            for d in deps:
                if d.dma:
                    key, val = ("d", d.dsem), d.dcnt
                else:
                    if not d.inc or self._skip(o, d):
                        continue
                    key, val = d.eng, d.cnt
                if kn.get(key, 0) >= val:
                    continue
                if need.get(key, 0) < val:
                    need[key] = val
            for key, val in need.items():
                if kn.get(key, 0) < val:
                    kn[key] = val
                s = snap.get((key, val))
                if s is not None:
                    for k2, v2 in s.items():
                        if kn.get(k2, 0) < v2:
                            kn[k2] = v2
            o.waits = list(need.items())
            if o.dma:
                snap[(("d", o.dsem), o.dcnt)] = dict(kn)
            elif o.inc:
                snap[(o.eng, o.cnt)] = dict(kn)
        self.stats = {e: (len(self.per[e]), cnt[e]) for e in ENGS}
        self.stats["dma"] = rr
        with contextlib.ExitStack() as st:
            esem = {e: st.enter_context(nc.semaphore("s_" + e)) for e in ENGS}
            dsem = [st.enter_context(nc.semaphore("d%d" % j)) for j in range(self.n_dma_sems)]
            block = st.enter_context(nc.Block())

            def run(eng_name):
                def body(e):
                    for o in self.per[eng_name]:
                        for key, val in o.waits:
                            s = dsem[key[1]] if isinstance(key, tuple) else esem[key]
                            e.wait_ge(s, val)
                        ins = o.fn(e)
                        if o.dma:
                            ins.then_inc(dsem[o.dsem], 16)
                        elif o.inc:
                            ins.then_inc(esem[eng_name], 1)
                return body

            block.tensor(run("pe"))
            block.scalar(run("act"))
            block.vector(run("dve"))
            block.gpsimd(run("pool"))
            block.sync(run("sp"))


def _prod(s):
    n = 1
    for v in s:
        n *= v
    return n


class Arena:
    def __init__(self, ap32, nwords):
        self.ap = ap32
        self.nw = nwords
        self.off = 0

    def reset(self, off=0):
        self.off = off

    def _take(self, words):
        a = self.off
        self.off += words
        assert self.off <= self.nw, ("arena overflow", self.off, self.nw)
        return self.ap[:, a:a + words]

    @staticmethod
    def _shape(ap, shape):
        if len(shape) == 1:
            return ap
        if len(shape) == 2:
            return ap.rearrange("p (a b) -> p a b", a=shape[0])
        return ap.rearrange("p (a b c) -> p a b c", a=shape[0], b=shape[1])

    def f32(self, *shape):
        return self._shape(self._take(_prod(shape)), shape)

    def bf16(self, *shape):
        n = _prod(shape)
        assert n % 2 == 0
        return self._shape(self._take(n // 2).bitcast(BF16), shape)


def _cmap():
    m = {}
    off = 0
    for name, n in (("cond", 8), ("flag", 1), ("flagm1", 1), ("bmod", 4 * 48), ("normg", 16 * 8),
                    ("cw", 3 * 248), ("ssd_cb", 2 * 32), ("ssd_dtb", 2), ("ssd_alog", 2), ("ssd_d", 2 * 16),
                    ("ssd_ng", 2 * 16), ("lam", 4 * 64), ("subg", 1), ("abias", 18 * 8),
                    ("ident", 128), ("ones", 128), ("rotp", 128), ("triU", 128), ("triL", 128),
                    ("LF", 128), ("LB", 128), ("maskF", 128), ("maskB", 128)):
        m[name] = (off, n)
        off += n
    return m, off


CMAP, NCONST = _cmap()
NCW = 248
CW_FFN, CW_SSD, CW_SC = 0, 176, 240
ARENA_WORDS = 19968


class Builder:
    def __init__(self, layers=(0, 1, 2, 3), mixers=None):
        self.layers = tuple(layers)
        self.mixers = set(layers) if mixers is None else set(mixers)
        self.nc = nc = bass.Bass("TRN2", target_bir_lowering=False)
        import os as _os
        self.sim = _os.environ.get("KSIM", "0") == "1"
        self.wq = "sp" if self.sim else "pool"
        WDT = BF16 if self.sim else F32
        self.S = Sched(nc)
        self.st = contextlib.ExitStack()

        def din(name, shape, dt=F32):
            return nc.dram_tensor(name, list(shape), dt, kind="ExternalInput").ap()

        def dout(name, shape, dt=F32):
            return nc.dram_tensor(name, list(shape), dt, kind="ExternalOutput").ap()

        def dscr(name, shape, dt):
            return nc.dram_tensor(name, list(shape), dt, kind="Internal").ap()

        self.d_x = din("xT", [128, KC, T])
        self.d_consts = din("consts", [128, NCONST])
        self.d_rot = din("rot", [128, 2, T])
        self.d_ssm0 = din("ssm0", [2, 2, 128, 2048])
        self.d_kc = din("kcache", [128, 8, 256], WDT)
        self.d_vc = din("vcache", [128, 2, 1024], WDT)
        self.d_wmod = din("wmod", [4, 12, 128, 8 * 512], WDT)
        self.d_fup = din("fup", [4, 11, 128, 8 * 512], WDT)
        self.d_fdn = din("fdn", [4, 4, 128, 22 * 256], WDT)
        self.d_sin = din("ssdin", [2, 12, 128, 8 * 512], WDT)
        self.d_sdt = din("ssddt", [2, 128, 8 * 64], WDT)
        self.d_sout = din("ssdout", [2, 4, 128, 16 * 256], WDT)
        self.d_scin = din("scin", [8, 128, 8 * 384], WDT)
        self.d_scout = din("scout", [128, 8 * 1024], WDT)
        self.d_qkv = din("qkv", [8, 128, 8 * 384], WDT)
        self.d_dout = din("daout", [128, 8 * 1024], WDT)
        self.o_y = dout("yT", [128, KC, T])
        self.o_st = dout("stout", [2, NSEG, 2, 128, 2048])
        self.o_k = dout("knew", [128, KC, T])
        self.o_v = dout("vnew", [128, 16, 1024])
        self.s_xbc = dscr("s_xbc", [16, 128, 32 * 128], BF16)
        self.s_z = dscr("s_z", [16, 128, 16 * 128], BF16)
        self.s_yf = dscr("s_yf", [16, 128, 2048], BF16)
        self.s_yn = dscr("s_yn", [16, 128, T], BF16)

        sb = lambda name, shape, dt: self.st.enter_context(nc.sbuf_tensor(name, list(shape), dt))
        self.xres = sb("xres", [128, KC, T], F32)
        self.hb = sb("hb", [128, KC, T], BF16)
        self.consts = sb("consts_sb", [128, NCONST], F32)
        self.rstd = sb("rstd", [128, T], F32)
        self.mod = sb("mod", [128, 4, 48], F32)
        self.der = sb("der", [128, 4, 4, 8], F32)
        self.cwd = sb("cwd", [128, 4, NCW], F32)
        self.scond = sb("scond", [128, 8], BF16)
        self.cbf = sb("cbf", [128, 9, 128], BF16)
        self.sq = sb("sq", [128, 2, 512], BF16)
        self.tmpf = sb("tmpf", [128, 2, 512], F32)
        self.misc = sb("misc", [128, 16], F32)
        self.hh = sb("hh", [128, KC, 2], BF16)
        arena_t = sb("arena", [128, ARENA_WORDS], F32)
        self.ar = Arena(arena_t[:, :], ARENA_WORDS)
        self.PS = self.st.enter_context(nc.psum_tensor("PS", [128, 8, 512], F32))
        self.sqi = 0
        self.tmi = 0

    def dbg(self, name, ap, keys, dt=F32):
        if not getattr(self, "debug", False):
            return
        shape = [int(v) for v in ap.shape]
        d = self.nc.dram_tensor("dbg_" + name, shape, dt, kind="ExternalOutput").ap()
        self.S.op("sp", lambda e: e.dma_start(out=d, in_=ap), reads=list(keys), dma=True, is_out=True)

    def c(self, name, a=None, b=None):
        off, n = CMAP[name]
        if a is None:
            return self.consts[:, off:off + n]
        return self.consts[:, off + a:off + (a + 1 if b is None else b)]

    def cw(self, tap, col):
        off, _ = CMAP["cw"]
        return self.consts[:, off + tap * NCW + col: off + tap * NCW + col + 1]

    def cwdv(self, which, col):
        return self.cwd[:, which, col:col + 1]

    def bank(self, b, n=1):
        return self.PS[:, b:b + n, :].rearrange("p b n -> p (b n)") if n > 1 else self.PS[:, b, :]

    def setup(self):
        S = self.S
        S.op("sp", lambda e: e.dma_start(out=self.consts[:, :], in_=self.d_consts), writes=[("c", "consts")], dma=True)
        for kc in range(KC):
            S.op("sp", lambda e, kc=kc: e.dma_start(out=self.xres[:, kc, :], in_=self.d_x[:, kc, :]),
                 writes=[("x", kc, tt) for tt in range(4)], dma=True)
        CC = [("c", "consts")]
        S.op("act", lambda e: e.activation(out=self.scond[:, :], in_=self.c("cond"), func=AF.Silu), reads=CC, writes=[("c", "scond")])
        for i, nm in enumerate(("ident", "ones", "triU", "triL", "LF", "LB", "maskF", "maskB")):
            S.op("dve", lambda e, i=i, nm=nm: e.tensor_copy(out=self.cbf[:, i, :], in_=self.c(nm)), reads=CC, writes=[("c", "cbf")])
        off = CMAP["cw"][0]
        cwall = self.consts[:, off:off + 3 * NCW].rearrange("p (a n) -> p a n", a=3)
        for w, (tap, fl) in enumerate(((0, "flag"), (2, "flag"), (0, "flagm1"), (2, "flagm1"))):
            S.op("dve", lambda e, w=w, tap=tap, fl=fl: e.tensor_scalar(out=self.cwd[:, w, :], in0=cwall[:, tap, :], scalar1=self.c(fl), scalar2=None, op0=ALU.mult),
                 reads=CC, writes=[("c", "cwd")])

    def ident(self):
        return self.cbf[:, 0, :]

    def ones(self):
        return self.cbf[:, 1, :]

    def phase_mod(self):
        S, ar = self.S, self.ar
        S.new_phase("A")
        ar.reset()
        wb = [ar.bf16(8, 512) for _ in range(2)]
        psm = self.PS[:, 7, :]
        for l in self.layers:
            for j in range(12):
                b = j % 2
                S.op(self.wq, lambda e, l=l, j=j, b=b: e.dma_start(out=wb[b], in_=self.d_wmod[l, j].rearrange("p (k n) -> p k n", k=8)),
                     writes=[("A", "w", b)], dma=True)
                for m in range(4):
                    col = l * 48 + j * 4 + m
                    for kc in range(KC):
                        S.op("pe", lambda e, b=b, m=m, kc=kc, col=col: e.matmul(psm[:, col:col + 1], lhsT=wb[b][:, kc, m * 128:(m + 1) * 128],
                                                                                 rhs=self.scond[:, kc:kc + 1], start=(kc == 0), stop=(kc == KC - 1)),
                             reads=[("A", "w", b), ("c", "scond")], writes=[("ps", 7)])
            bo = CMAP["bmod"][0]
            S.op("dve", lambda e, l=l, bo=bo: e.tensor_tensor(out=self.mod[:, l, :], in0=psm[:, l * 48:(l + 1) * 48],
                                                              in1=self.consts[:, bo + l * 48: bo + (l + 1) * 48], op=ALU.add),
                 reads=[("ps", 7), ("c", "consts")], writes=[("c", "mod", l)])
            go = CMAP["normg"][0]
            g = lambda i, l=l: self.consts[:, go + (l * 4 + i) * 8: go + (l * 4 + i + 1) * 8]
            mo = lambda i, l=l: self.mod[:, l, i * 8:(i + 1) * 8]
            for di, (mi, gi, plus1) in enumerate(((1, 0, True), (2, 1, False), (4, 2, True), (5, 3, False))):
                if plus1:
                    S.op("dve", lambda e, l=l, di=di, mi=mi, gi=gi: e.scalar_tensor_tensor(out=self.der[:, l, di, :], in0=mo(mi), scalar=1.0, in1=g(gi), op0=ALU.add, op1=ALU.mult),
                         reads=[("c", "mod", l), ("c", "consts")], writes=[("c", "der", l, di)])
                else:
                    S.op("dve", lambda e, l=l, di=di, mi=mi, gi=gi: e.tensor_tensor(out=self.der[:, l, di, :], in0=mo(mi), in1=g(gi), op=ALU.mult),
                         reads=[("c", "mod", l), ("c", "consts")], writes=[("c", "der", l, di)])

    def shift(self, l, which):
        return self.mod[:, l, which * 8:(which + 1) * 8]

    def _sumsq_rstd(self, src_ap_fn, src_keys_fn, tt, bankno, on_dve=False):
        S = self.S
        ts = slice(tt * 512, (tt + 1) * 512)
        for kc in range(KC):
            i = self.sqi % 2
            self.sqi += 1
            if on_dve and kc % 2 == 1:
                S.op("dve", lambda e, kc=kc, i=i: e.tensor_tensor(out=self.sq[:, i, :], in0=src_ap_fn(kc, ts), in1=src_ap_fn(kc, ts), op=ALU.mult),
                     reads=src_keys_fn(kc, tt), writes=[("c", "sq", i)])
            else:
                S.op("act", lambda e, kc=kc, i=i: e.activation(out=self.sq[:, i, :], in_=src_ap_fn(kc, ts), func=AF.Square),
                     reads=src_keys_fn(kc, tt), writes=[("c", "sq", i)])
            S.op("pe", lambda e, kc=kc, i=i: e.matmul(self.PS[:, bankno, :], lhsT=self.ones(), rhs=self.sq[:, i, :], start=(kc == 0), stop=(kc == KC - 1)),
                 reads=[("c", "sq", i), ("c", "cbf")], writes=[("ps", bankno)])
        S.op("act", lambda e: e.activation(out=self.rstd[:, ts], in_=self.PS[:, bankno, :], func=AF.Sqrt, scale=1.0 / D, bias=EPS),
             reads=[("ps", bankno)], writes=[("c", "rstd", tt)])
        S.op("dve", lambda e: e.reciprocal(out=self.rstd[:, ts], in_=self.rstd[:, ts]), reads=[("c", "rstd", tt)], writes=[("c", "rstd", tt)])

    def prenorm(self, l, which, tts=range(4)):
        S = self.S
        di = 0 if which == 0 else 2
        shi = 0 if which == 0 else 3
        for tt in tts:
            ts = slice(tt * 512, (tt + 1) * 512)
            bankno = 6 + (tt % 2)
            self._sumsq_rstd(lambda kc, ts: self.xres[:, kc, ts], lambda kc, tt: [("x", kc, tt)], tt, bankno, on_dve=False)
            for kc in range(KC):
                i = self.tmi % 2
                self.tmi += 1
                S.op("dve", lambda e, kc=kc, i=i, ts=ts: e.scalar_tensor_tensor(out=self.tmpf[:, i, :], in0=self.xres[:, kc, ts], scalar=self.der[:, l, di, kc:kc + 1],
                                                                                  in1=self.rstd[:, ts], op0=ALU.mult, op1=ALU.mult),
                     reads=[("x", kc, tt), ("c", "der", l, di), ("c", "rstd", tt)], writes=[("c", "tmpf", i)])
                S.op("act", lambda e, kc=kc, i=i, ts=ts: e.activation(out=self.hb[:, kc, ts], in_=self.tmpf[:, i, :], func=AF.Identity,
                                                                       bias=self.mod[:, l, shi * 8 + kc: shi * 8 + kc + 1], scale=1.0),
                     reads=[("c", "tmpf", i), ("c", "mod", l)], writes=[("hb", kc, tt)])

    def postnorm(self, l, which, tts=range(4)):
        S = self.S
        di = 1 if which == 0 else 3
        for tt in tts:
            ts = slice(tt * 512, (tt + 1) * 512)
            bankno = 6 + (tt % 2)
            self._sumsq_rstd(lambda kc, ts: self.hb[:, kc, ts], lambda kc, tt: [("hb", kc, tt)], tt, bankno)
            for kc in range(KC):
                i = self.tmi % 2
                self.tmi += 1
                S.op("dve", lambda e, kc=kc, i=i, ts=ts: e.scalar_tensor_tensor(out=self.tmpf[:, i, :], in0=self.hb[:, kc, ts], scalar=self.der[:, l, di, kc:kc + 1],
                                                                                  in1=self.rstd[:, ts], op0=ALU.mult, op1=ALU.mult),
                     reads=[("hb", kc, tt), ("c", "der", l, di), ("c", "rstd", tt)], writes=[("c", "tmpf", i)])
                S.op("dve", lambda e, kc=kc, i=i, ts=ts: e.tensor_tensor(out=self.xres[:, kc, ts], in0=self.tmpf[:, i, :], in1=self.xres[:, kc, ts], op=ALU.add),
                     reads=[("c", "tmpf", i), ("x", kc, tt)], writes=[("x", kc, tt)])

    def conv3(self, src, src_keys, acc, acc_key, ccol, n, halo_l=None, halo_r=None, halo_keys=()):
        S = self.S
        R = list(src_keys) + [("c", "consts"), ("c", "cwd")]
        W = [acc_key]
        S.op("act", lambda e: e.activation(out=acc[:, 0:n], in_=src[:, 0:n], func=AF.Identity, scale=self.cw(1, ccol)), reads=R, writes=W)
        S.op("dve", lambda e: e.scalar_tensor_tensor(out=acc[:, 1:n], in0=src[:, 0:n - 1], scalar=self.cw(0, ccol), in1=acc[:, 1:n], op0=ALU.mult, op1=ALU.add),
             reads=R + W, writes=W)
        S.op("dve", lambda e: e.scalar_tensor_tensor(out=acc[:, 0:n - 1], in0=src[:, 1:n], scalar=self.cw(2, ccol), in1=acc[:, 0:n - 1], op0=ALU.mult, op1=ALU.add),
             reads=R + W, writes=W)
        nb = n // SEGL - 1
        S.op("dve", lambda e: e.scalar_tensor_tensor(out=acc[:, SEGL:n:SEGL], in0=src[:, SEGL - 1:n - 1:SEGL], scalar=self.cwdv(2, ccol), in1=acc[:, SEGL:n:SEGL], op0=ALU.mult, op1=ALU.add),
             reads=R + W, writes=W)
        S.op("dve", lambda e: e.scalar_tensor_tensor(out=acc[:, SEGL - 1:n - 1:SEGL], in0=src[:, SEGL:n:SEGL], scalar=self.cwdv(3, ccol), in1=acc[:, SEGL - 1:n - 1:SEGL], op0=ALU.mult, op1=ALU.add),
             reads=R + W, writes=W)
        if halo_l is not None:
            S.op("dve", lambda e: e.scalar_tensor_tensor(out=acc[:, 0:1], in0=halo_l, scalar=self.cwdv(0, ccol), in1=acc[:, 0:1], op0=ALU.mult, op1=ALU.add),
                 reads=R + W + list(halo_keys), writes=W)
        if halo_r is not None:
            S.op("dve", lambda e: e.scalar_tensor_tensor(out=acc[:, n - 1:n], in0=halo_r, scalar=self.cwdv(1, ccol), in1=acc[:, n - 1:n], op0=ALU.mult, op1=ALU.add),
                 reads=R + W + list(halo_keys), writes=W)

    def ffn(self, l):
        S, ar = self.S, self.ar
        TH = 1024
        self.prenorm(l, 1)
        S.op("dve", lambda e: e.tensor_copy(out=self.hh[:, :, :], in_=self.hb[:, :, TH - 1:TH + 1]),
             reads=[("hb", kc, tt) for kc in range(KC) for tt in (1, 2)], writes=[("c", "hh")])
        self.dbg("mod%d" % l, self.mod[:, l, :], [("c", "mod", l)])
        self.dbg("der%d" % l, self.der[:, l, :, :], [("c", "der", l, i) for i in range(4)])
        self.dbg("h%d" % l, self.hb[:, :, 0:512], [("hb", kc, 0) for kc in range(KC)], BF16)
        self.dbg("rstd%d" % l, self.rstd[:, 0:512], [("c", "rstd", 0)])
        for th in range(2):
            S.new_phase("A")
            S.new_phase("B")
            ar.reset()
            hid = ar.bf16(NPAIR, TH)
            markB = ar.off
            wu = [ar.bf16(8, 512) for _ in range(2)]
            accg = [ar.f32(TH) for _ in range(2)]
            accv = [ar.f32(TH) for _ in range(2)]
            hcol = 1 if th == 0 else 0
            ci = 0
            for pg in range(11):
                b = pg % 2
                S.op(self.wq, lambda e, pg=pg, b=b: e.dma_start(out=wu[b], in_=self.d_fup[l, pg].rearrange("p (k n) -> p k n", k=8)),
                     writes=[("B", "wu", b)], dma=True)
                for pi in range(2):
                    for isval in (0, 1):
                        c = pi + 2 * isval
                        hf = ci % 2
                        ci += 1
                        b0 = hf * 3
                        for t2 in range(2):
                            for kc in range(KC):
                                S.op("pe", lambda e, b=b, c=c, kc=kc, t2=t2, b0=b0: e.matmul(self.PS[:, b0 + t2, :], lhsT=wu[b][:, kc, c * 128:(c + 1) * 128],
                                                                                              rhs=self.hb[:, kc, th * TH + t2 * 512: th * TH + (t2 + 1) * 512],
                                                                                              start=(kc == 0), stop=(kc == KC - 1)),
                                     reads=[("B", "wu", b), ("hb", kc, 2 * th + t2)], writes=[("ps", b0 + t2)])
                        for kc in range(KC):
                            S.op("pe", lambda e, b=b, c=c, kc=kc, b0=b0: e.matmul(self.PS[:, b0 + 2, 0:1], lhsT=wu[b][:, kc, c * 128:(c + 1) * 128],
                                                                                   rhs=self.hh[:, kc, hcol:hcol + 1], start=(kc == 0), stop=(kc == KC - 1)),
                                 reads=[("B", "wu", b), ("c", "hh")], writes=[("ps", b0 + 2)])
                        ccol = CW_FFN + l * 44 + (22 if isval else 0) + 2 * pg + pi
                        acc = (accv if isval else accg)[pi]
                        akey = ("B", "accv" if isval else "accg", pi)
                        src = self.bank(b0, 2)
                        halo = self.PS[:, b0 + 2, 0:1]
                        self.conv3(src, [("ps", b0), ("ps", b0 + 1)], acc, akey, ccol, TH,
                                   halo_l=(halo if th == 1 else None), halo_r=(halo if th == 0 else None), halo_keys=[("ps", b0 + 2)])
                    j = 2 * pg + pi
                    S.op("act", lambda e, pi=pi: e.activation(out=accg[pi], in_=accg[pi], func=AF.Silu), reads=[("B", "accg", pi)], writes=[("B", "accg", pi)])
                    S.op("dve", lambda e, pi=pi, j=j: e.tensor_tensor(out=hid[:, j, :], in0=accg[pi], in1=accv[pi], op=ALU.mult),
                         reads=[("B", "accg", pi), ("B", "accv", pi)], writes=[("A", "hid", j)])
            if th == 0:
                self.dbg("hid%d" % l, hid[:, 0:2, :], [("A", "hid", 0), ("A", "hid", 1)], BF16)
                self.dbg("accv%d" % l, accv[0], [("B", "accv", 0)])
            S.new_phase("B")
            ar.reset(markB)
            wd = [ar.bf16(NPAIR, 256) for _ in range(2)]
            for mt in range(4):
                b = mt % 2
                S.op(self.wq, lambda e, mt=mt, b=b: e.dma_start(out=wd[b], in_=self.d_fdn[l, mt].rearrange("p (k n) -> p k n", k=NPAIR)),
                     writes=[("B", "wd", b)], dma=True)
                for mc in range(2):
                    m = mt * 2 + mc
                    b0 = (m % 2) * 3
                    for t2 in range(2):
                        for kc in range(NPAIR):
                            S.op("pe", lambda e, b=b, mc=mc, kc=kc, t2=t2, b0=b0: e.matmul(self.PS[:, b0 + t2, :], lhsT=wd[b][:, kc, mc * 128:(mc + 1) * 128],
                                                                                            rhs=hid[:, kc, t2 * 512:(t2 + 1) * 512], start=(kc == 0), stop=(kc == NPAIR - 1)),
                                 reads=[("B", "wd", b), ("A", "hid", kc)], writes=[("ps", b0 + t2)])
                    S.op("act", lambda e, m=m, b0=b0: e.activation(out=self.hb[:, m, th * TH:(th + 1) * TH], in_=self.bank(b0, 2), func=AF.Identity),
                         reads=[("ps", b0), ("ps", b0 + 1)], writes=[("hb", m, 2 * th), ("hb", m, 2 * th + 1)])
            if th == 0:
                self.dbg("f%d" % l, self.hb[:, :, 0:512], [("hb", kc, 0) for kc in range(KC)], BF16)
            self.postnorm(l, 1, tts=[2 * th, 2 * th + 1])

    def mixer_sconv(self, l):
        S, ar = self.S, self.ar
        S.new_phase("A")
        S.new_phase("B")
        ar.reset()
        gbuf = ar.bf16(KC, T)
        markB = ar.off
        wt = [ar.bf16(8, 384) for _ in range(2)]
        cgs = ar.f32(T)
        pr = ar.f32(T)
        acc = ar.f32(T)
        self.prenorm(l, 0)

        def proj(i, b, colblk, b0):
            for tt in range(4):
                for kc in range(KC):
                    S.op("pe", lambda e, kc=kc, tt=tt: e.matmul(self.PS[:, b0 + tt, :], lhsT=wt[b][:, kc, colblk * 128:(colblk + 1) * 128],
                                                                 rhs=self.hb[:, kc, tt * 512:(tt + 1) * 512], start=(kc == 0), stop=(kc == KC - 1)),
                         reads=[("B", "wt", b), ("hb", kc, tt)], writes=[("ps", b0 + tt)])

        def pk(b0):
            return [("ps", b0 + t) for t in range(4)]

        for i in range(KC):
            b = i % 2
            S.op(self.wq, lambda e, i=i, b=b: e.dma_start(out=wt[b], in_=self.d_scin[i].rearrange("p (k n) -> p k n", k=8)), writes=[("B", "wt", b)], dma=True)
            proj(i, b, 1, 0)
            S.op("act", lambda e: e.activation(out=cgs, in_=self.bank(0, 4), func=AF.Identity), reads=pk(0), writes=[("B", "cgs")])
            proj(i, b, 2, 4)
            S.op("dve", lambda e: e.tensor_tensor(out=pr, in0=cgs, in1=self.bank(4, 4), op=ALU.mult), reads=[("B", "cgs")] + pk(4), writes=[("B", "pr")])
            self.conv3(pr, [("B", "pr")], acc, ("B", "acc"), CW_SC + i, T)
            proj(i, b, 0, 0)
            S.op("dve", lambda e, i=i: e.tensor_tensor(out=gbuf[:, i, :], in0=acc, in1=self.bank(0, 4), op=ALU.mult), reads=[("B", "acc")] + pk(0), writes=[("A", "g", i)])
        S.new_phase("B")
        ar.reset(markB)
        wo = ar.bf16(8, 1024)
        S.op(self.wq, lambda e: e.dma_start(out=wo, in_=self.d_scout.rearrange("p (k n) -> p k n", k=8)), writes=[("B", "wo")], dma=True)
        for mc in range(KC):
            b0 = (mc % 2) * 4
            for tt in range(4):
                for kc in range(KC):
                    S.op("pe", lambda e, kc=kc, tt=tt, mc=mc, b0=b0: e.matmul(self.PS[:, b0 + tt, :], lhsT=wo[:, kc, mc * 128:(mc + 1) * 128],
                                                                              rhs=gbuf[:, kc, tt * 512:(tt + 1) * 512], start=(kc == 0), stop=(kc == KC - 1)),
                         reads=[("B", "wo"), ("A", "g", kc)], writes=[("ps", b0 + tt)])
            S.op("act", lambda e, mc=mc, b0=b0: e.activation(out=self.hb[:, mc, :], in_=self.bank(b0, 4), func=AF.Identity),
                 reads=pk(b0), writes=[("hb", mc, tt) for tt in range(4)])
        self.postnorm(l, 0)

    def build(self):
        self.setup()
        self.phase_mod()
        for l in self.layers:
            if l in self.mixers:
                kind = l % 3
                if kind == 0:
                    self.mixer_ssd(l)
                elif kind == 1:
                    self.mixer_sconv(l)
                else:
                    self.mixer_attn(l)
            self.ffn(l)
        self.finish()

    def finish(self):
        S = self.S
        for kc in range(KC):
            S.op("sp", lambda e, kc=kc: e.dma_start(out=self.o_y[:, kc, :], in_=self.xres[:, kc, :]),
                 reads=[("x", kc, tt) for tt in range(4)], dma=True, is_out=True)
        S.emit()
        self.st.close()


def mixer_ssd(self, l):
    S, ar = self.S, self.ar
    j = l // 3
    AX = mybir.AxisListType.X
    CC = [("c", "consts")]
    CB = [("c", "cbf")]
    S.new_phase("A")
    S.new_phase("B")
    ar.reset()
    dt_all = ar.f32(16, 64)
    dta_all = ar.f32(16, 64)
    hT = ar.f32(2048)
    hTb = ar.bf16(2048)
    mask4 = ar.bf16(4, 128)
    small = ar.f32(8, 32)
    hilo = ar.bf16(2, 32)
    markB = ar.off
    self.prenorm(l, 0)
    wt = [ar.bf16(8, 512) for _ in range(2)]
    acc = ar.f32(T)
    stage = [ar.bf16(T) for _ in range(2)]
    dtT = ar.f32(T)
    dtAT = ar.f32(T)
    wdt = ar.bf16(8, 64)
    z_dst = self.s_z.rearrange("c p n -> p c n")
    x_dst = self.s_xbc.rearrange("c p n -> p c n")
    sti = 0
    for i in range(12):
        b = i % 2
        S.op(self.wq, lambda e, i=i, b=b: e.dma_start(out=wt[b], in_=self.d_sin[j, i].rearrange("p (k n) -> p k n", k=8)), writes=[("B", "wt", b)], dma=True)
        for c in range(4):
            f = 4 * i + c
            b0 = (f % 2) * 4
            for tt in range(4):
                for kc in range(KC):
                    S.op("pe", lambda e, b=b, c=c, kc=kc, tt=tt, b0=b0: e.matmul(self.PS[:, b0 + tt, :], lhsT=wt[b][:, kc, c * 128:(c + 1) * 128],
                                                                                  rhs=self.hb[:, kc, tt * 512:(tt + 1) * 512], start=(kc == 0), stop=(kc == KC - 1)),
                         reads=[("B", "wt", b), ("hb", kc, tt)], writes=[("ps", b0 + tt)])
            pk = [("ps", b0 + t) for t in range(4)]
            si = sti % 2
            sti += 1
            st_ = stage[si]
            if f < 16:
                S.op("act", lambda e, b0=b0, st_=st_: e.activation(out=st_, in_=self.bank(b0, 4), func=AF.Silu), reads=pk, writes=[("B", "stage", si)])
                S.op("sp", lambda e, f=f, st_=st_: e.dma_start(out=z_dst[:, :, f * 128:(f + 1) * 128], in_=st_.rearrange("p (c t) -> p c t", c=16)),
                     reads=[("B", "stage", si)], writes=[("dr", "z", f)], dma=True)
            else:
                fx = f - 16
                self.conv3(self.bank(b0, 4), pk, acc, ("B", "acc"), CW_SSD + j * 32 + fx, T)
                cbo = CMAP["ssd_cb"][0] + j * 32 + fx
                S.op("act", lambda e, st_=st_, cbo=cbo: e.activation(out=st_, in_=acc, func=AF.Silu, bias=self.consts[:, cbo:cbo + 1], scale=1.0),
                     reads=[("B", "acc")] + CC, writes=[("B", "stage", si)])
                S.op("sp", lambda e, fx=fx, st_=st_: e.dma_start(out=x_dst[:, :, fx * 128:(fx + 1) * 128], in_=st_.rearrange("p (c t) -> p c t", c=16)),
                     reads=[("B", "stage", si)], writes=[("dr", "xbc", fx)], dma=True)
    S.op(self.wq, lambda e: e.dma_start(out=wdt, in_=self.d_sdt[j].rearrange("p (k n) -> p k n", k=8)), writes=[("B", "wdt")], dma=True)
    for tt in range(4):
        for kc in range(KC):
            S.op("pe", lambda e, kc=kc, tt=tt: e.matmul(self.PS[0:64, tt, :], lhsT=wdt[:, kc, :], rhs=self.hb[:, kc, tt * 512:(tt + 1) * 512], start=(kc == 0), stop=(kc == KC - 1)),
                 reads=[("B", "wdt"), ("hb", kc, tt)], writes=[("ps", tt)])
    pk = [("ps", t) for t in range(4)]
    dbo = CMAP["ssd_dtb"][0] + j
    alo = CMAP["ssd_alog"][0] + j
    ps64 = self.PS[0:64, 0:4, :].rearrange("p b n -> p (b n)")
    S.op("act", lambda e: e.activation(out=dtT[0:64, :], in_=ps64, func=AF.Exp, bias=self.consts[0:64, dbo:dbo + 1], scale=1.0), reads=pk + CC, writes=[("B", "dtT")])
    S.op("act", lambda e: e.activation(out=dtT[0:64, :], in_=dtT[0:64, :], func=AF.Ln, bias=1.0, scale=1.0), reads=[("B", "dtT")], writes=[("B", "dtT")])
    S.op("act", lambda e: e.activation(out=self.misc[0:64, 8:9], in_=self.consts[0:64, alo:alo + 1], func=AF.Exp), reads=CC, writes=[("c", "misc")])
    S.op("dve", lambda e: e.tensor_scalar(out=self.misc[0:64, 8:9], in0=self.misc[0:64, 8:9], scalar1=-1.0, scalar2=None, op0=ALU.mult), reads=[("c", "misc")], writes=[("c", "misc")])
    S.op("dve", lambda e: e.tensor_scalar(out=dtAT[0:64, :], in0=dtT[0:64, :], scalar1=self.misc[0:64, 8:9], scalar2=None, op0=ALU.mult),
         reads=[("B", "dtT"), ("c", "misc")], writes=[("B", "dtAT")])
    id32 = self.c("ident")
    for which, (src_, dst_, key) in enumerate(((dtT, dt_all, ("B", "dtT")), (dtAT, dta_all, ("B", "dtAT")))):
        for tc in range(16):
            bk = 4 + which * 2 + tc // 8
            off = (tc % 8) * 64
            S.op("pe", lambda e, src_=src_, tc=tc, bk=bk, off=off: e.matmul(self.PS[:, bk, off:off + 64], lhsT=src_[0:64, tc * 128:(tc + 1) * 128], rhs=id32[0:64, 0:64], start=True, stop=True),
                 reads=[key] + CC, writes=[("ps", bk)])
        for hb_ in range(2):
            bk = 4 + which * 2 + hb_
            S.op("act", lambda e, dst_=dst_, bk=bk, hb_=hb_: e.activation(out=dst_[:, hb_ * 8:(hb_ + 1) * 8, :], in_=self.PS[:, bk, :].rearrange("p (a b) -> p a b", a=8), func=AF.Identity),
                 reads=[("ps", bk)], writes=[("A", "dtall")])
    for d in range(2):
        S.new_phase("B")
        ar.reset(markB)
        xb = [ar.bf16(32, 128) for _ in range(2)]
        b_tok = ar.bf16(1024)
        xdt = ar.bf16(2048)
        xw = ar.bf16(2048)
        Rhl = [[ar.bf16(4, 128) for _ in range(2)] for _ in range(3)]
        Et = [ar.bf16(4, 128) for _ in range(3)]
        Mt = [ar.bf16(4, 128) for _ in range(3)]
        cbs = [ar.bf16(128) for _ in range(3)]
        if d == 1:
            zT = ar.bf16(16, 128)
            yf = ar.bf16(2048)
            gat = ar.bf16(16, 128)
            tmp8 = ar.f32(8, 128)
            sqb = self.sq[:, :, :].rearrange("p a (b t) -> p (a b) t", t=128)
            rs = ar.f32(128)
            ynT = gat
        else:
            ychb = ar.bf16(2048)
        tri32 = self.c("triU" if d == 0 else "triL")
        one32 = self.c("ones")
        triq = self.cbf[:, 2 + d, :]
        Lm = self.cbf[:, 4 + d, :]
        mk = self.cbf[:, 6 + d, :]
        S.op("dve", lambda e, mk=mk: e.tensor_copy(out=mask4, in_=mk.unsqueeze(1).broadcast_to([128, 4, 128])), reads=CB, writes=[("A", "mask4")])
        S.op("sp", lambda e, d=d: e.dma_start(out=hT, in_=self.d_ssm0[j, d]), writes=[("A", "hT")], dma=True)
        hs = slice(d * 32, (d + 1) * 32)
        order = list(range(16)) if d == 0 else list(range(15, -1, -1))

        def load_x(ci):
            S.op("sp", lambda e: e.dma_start(out=xb[ci % 2], in_=self.s_xbc[order[ci]].rearrange("p (c t) -> p c t", c=32)),
                 reads=[("dr", "xbc", f_) for f_ in range(32)], writes=[("B", "xbcT", ci % 2)], dma=True)

        load_x(0)
        pending_reset = False
        for ci, tc in enumerate(order):
            if ci + 1 < 16:
                load_x(ci + 1)
            xbcT = xb[ci % 2]
            XK = ("B", "xbcT", ci % 2)
            if d == 1:
                S.op("sp", lambda e, tc=tc: e.dma_start(out=zT, in_=self.s_z[tc].rearrange("p (c t) -> p c t", c=16)), reads=[("dr", "z", f_) for f_ in range(16)], writes=[("B", "zT")], dma=True)
                S.op("sp", lambda e, tc=tc: e.dma_start(out=yf, in_=self.s_yf[tc]), reads=[("dr", "yf", tc)], writes=[("B", "yf", g_) for g_ in range(8)] + [("B", "yo", g_) for g_ in range(8)], dma=True)
            if pending_reset:
                S.op("act", lambda e: e.activation(out=hTb, in_=hT, func=AF.Identity, scale=self.c("flag")), reads=[("A", "hT")] + CC, writes=[("A", "hTb")])
            else:
                S.op("act", lambda e: e.activation(out=hTb, in_=hT, func=AF.Identity), reads=[("A", "hT")], writes=[("A", "hTb")])
            for fc in range(24):
                bk = fc // 8
                psb = self.PS[:, bk, :].bitcast(BF16)
                S.op("pe", lambda e, fc=fc, psb=psb: e.transpose(out=psb[:, (fc % 8) * 128:(fc % 8 + 1) * 128], in_=xbcT[:, fc, :], identity=self.ident()),
                     reads=[XK] + CB, writes=[("ps", bk)])
            S.op("act", lambda e: e.activation(out=b_tok, in_=self.PS[:, 2, :].bitcast(BF16), func=AF.Identity), reads=[("ps", 2)], writes=[("B", "btok")])
            dta = dta_all[:, tc, hs]
            dtk = dt_all[:, tc, hs]
            S.op("pe", lambda e, dta=dta: e.matmul(self.PS[:, 3, 0:32], lhsT=tri32, rhs=dta, start=True, stop=True), reads=[("A", "dtall")] + CC, writes=[("ps", 3)])
            S.op("pe", lambda e, dta=dta: e.matmul(self.PS[:, 3, 32:64], lhsT=one32, rhs=dta, start=True, stop=True), reads=[("A", "dtall")] + CC, writes=[("ps", 3)])
            SK = [("B", "small")]
            S.op("act", lambda e: e.activation(out=small[:, 0, :], in_=self.PS[:, 3, 0:32], func=AF.Identity), reads=[("ps", 3)], writes=SK)
            S.op("act", lambda e: e.activation(out=small[:, 1, :], in_=self.PS[:, 3, 0:32], func=AF.Exp), reads=[("ps", 3)], writes=SK)
            S.op("act", lambda e: e.activation(out=small[:, 3, :], in_=self.PS[:, 3, 32:64], func=AF.Exp), reads=[("ps", 3)], writes=SK)
            S.op("dve", lambda e: e.tensor_tensor(out=small[:, 4, :], in0=self.PS[:, 3, 32:64], in1=small[:, 0, :], op=ALU.subtract), reads=[("ps", 3)] + SK, writes=SK)
            S.op("act", lambda e: e.activation(out=small[:, 2, :], in_=small[:, 4, :], func=AF.Exp), reads=SK, writes=SK)
            S.op("dve", lambda e, dta=dta: e.tensor_copy(out=hilo[:, 0, :], in_=dta), reads=[("A", "dtall")], writes=[("B", "hilo")])
            S.op("dve", lambda e, dta=dta: e.tensor_tensor(out=hilo[:, 1, :], in0=dta, in1=hilo[:, 0, :], op=ALU.subtract), reads=[("A", "dtall"), ("B", "hilo")], writes=[("B", "hilo")])
            S.op("dve", lambda e: e.tensor_copy(out=small[:, 6, :], in_=hilo[:, 0, :]), reads=[("B", "hilo")], writes=[("B", "hi32")])
            for bk in range(2):
                S.op("dve", lambda e, bk=bk, dtk=dtk: e.tensor_tensor(out=xdt[:, bk * 1024:(bk + 1) * 1024].rearrange("p (h q) -> p h q", h=16),
                                                                       in0=self.PS[:, bk, :].bitcast(BF16).rearrange("p (h q) -> p h q", h=16),
                                                                       in1=dtk[:, bk * 16:(bk + 1) * 16].unsqueeze(2).broadcast_to([128, 16, 64]), op=ALU.mult),
                     reads=[("ps", bk), ("A", "dtall")], writes=[("B", "xdt")])
            S.op("dve", lambda e: e.tensor_tensor(out=xw.rearrange("p (h q) -> p h q", h=32), in0=xdt.rearrange("p (h q) -> p h q", h=32), in1=small[:, 2, :].unsqueeze(2).broadcast_to([128, 32, 64]), op=ALU.mult),
                 reads=[("B", "xdt")] + SK, writes=[("B", "xw")])
            E2 = lambda a: a.rearrange("p a b -> p (a b)")
            ydst = ychb if d == 0 else yf

            def front(g):
                gb = g % 3
                R = Rhl[gb]
                dbk = (4, 6, 0)[gb]
                cb = self.PS[:, 3, 128 + gb * 128: 256 + gb * 128]
                for h4 in range(4):
                    S.op("act", lambda e, h4=h4: e.activation(out=R[0][:, h4, :], in_=triq, func=AF.Identity, scale=small[:, 6, 4 * g + h4:4 * g + h4 + 1]),
                         reads=CB + [("B", "hi32")], writes=[("B", "R", gb, 0)])
                S.op("dve", lambda e: e.tensor_tensor(out=R[1], in0=triq.unsqueeze(1).broadcast_to([128, 4, 128]),
                                                       in1=hilo[:, 1, 4 * g:4 * g + 4].unsqueeze(2).broadcast_to([128, 4, 128]), op=ALU.mult),
                     reads=CB + [("B", "hilo")], writes=[("B", "R", gb, 1)])
                S.op("pe", lambda e: e.matmul(self.PS[:, dbk, :], lhsT=Lm, rhs=E2(R[0]), start=True, stop=False), reads=CB + [("B", "R", gb, 0)], writes=[("ps", dbk)])
                S.op("pe", lambda e: e.matmul(self.PS[:, dbk, :], lhsT=Lm, rhs=E2(R[1]), start=False, stop=False), reads=CB + [("B", "R", gb, 1)], writes=[("ps", dbk)])
                S.op("pe", lambda e: e.matmul(self.PS[:, dbk, :], lhsT=self.ident(), rhs=E2(mask4), start=False, stop=True), reads=CB + [("A", "mask4")], writes=[("ps", dbk)])
                S.op("act", lambda e: e.activation(out=E2(Et[gb]), in_=self.PS[:, dbk, :], func=AF.Exp), reads=[("ps", dbk)], writes=[("B", "Et", gb)])
                S.op("pe", lambda e: e.matmul(cb, lhsT=xbcT[:, 16 + g, :], rhs=xbcT[:, 24 + g, :], start=True, stop=True), reads=[XK], writes=[("ps", 3)])
                S.op("act", lambda e: e.activation(out=cbs[gb], in_=cb, func=AF.Identity), reads=[("ps", 3)], writes=[("B", "cbs", gb)])

            def back(g):
                gb = g % 3
                cb = self.PS[:, 3, 128 + gb * 128: 256 + gb * 128]
                S.op("dve", lambda e: e.tensor_tensor(out=Mt[gb], in0=Et[gb], in1=cbs[gb].unsqueeze(1).broadcast_to([128, 4, 128]), op=ALU.mult),
                     reads=[("B", "Et", gb), ("B", "cbs", gb)], writes=[("B", "Mt", gb)])
                ybk = (5, 7, 1)[gb]
                tcb = (self.tmpf[:, 0, 0:256], self.tmpf[:, 1, 0:256], self.tmpf[:, 0, 256:512])[gb]
                tkey = ("c", "tmpf", gb % 2)
                if d == 1:
                    S.op("pe", lambda e: e.matmul(self.PS[:, ybk, 0:256], lhsT=self.ident(), rhs=yf[:, g * 256:(g + 1) * 256], start=True, stop=False),
                         reads=CB + [("B", "yf", g)], writes=[("ps", ybk)])
                for h4 in range(4):
                    h = 4 * g + h4
                    S.op("pe", lambda e, h4=h4, h=h: e.matmul(self.PS[:, ybk, h4 * 64:(h4 + 1) * 64], lhsT=Mt[gb][:, h4, :], rhs=xdt[:, h * 64:(h + 1) * 64],
                                                               start=(d == 0), stop=(d == 0 or h4 == 3)),
                         reads=[("B", "Mt", gb), ("B", "xdt")], writes=[("ps", ybk)])
                S.op("pe", lambda e: e.matmul(self.PS[:, ybk, 256:512], lhsT=xbcT[:, 24 + g, :], rhs=hTb[:, g * 256:(g + 1) * 256], start=True, stop=True),
                     reads=[XK, ("A", "hTb")], writes=[("ps", ybk)])
                S.op("dve", lambda e: e.tensor_tensor(out=tcb.rearrange("p (h q) -> p h q", h=4), in0=self.PS[:, ybk, 256:512].rearrange("p (h q) -> p h q", h=4),
                                                       in1=small[:, 1, 4 * g:4 * g + 4].unsqueeze(2).broadcast_to([128, 4, 64]), op=ALU.mult),
                     reads=[("ps", ybk)] + SK, writes=[tkey])
                S.op("dve", lambda e: e.tensor_tensor(out=ydst[:, g * 256:(g + 1) * 256], in0=tcb, in1=self.PS[:, ybk, 0:256], op=ALU.add),
                     reads=[("ps", ybk), tkey], writes=[("B", "yo", g)] + ([("B", "yf", g)] if d == 1 else []))

            front(0)
            front(1)
            for g in range(8):
                if g + 2 < 8:
                    front(g + 2)
                back(g)
            for g in range(8):
                bk, off = g // 2, (g % 2) * 256
                S.op("pe", lambda e, g=g, bk=bk, off=off: e.matmul(self.PS[:, bk, off:off + 256], lhsT=b_tok[:, g * 128:(g + 1) * 128], rhs=xw[:, g * 256:(g + 1) * 256], start=True, stop=True),
                     reads=[("B", "btok"), ("B", "xw")], writes=[("ps", bk)])
            h3 = hT.rearrange("p (h q) -> p h q", h=32)
            if pending_reset:
                S.op("dve", lambda e: e.tensor_scalar(out=small[:, 3, :], in0=small[:, 3, :], scalar1=self.c("flag"), scalar2=None, op0=ALU.mult), reads=SK + CC, writes=SK)
                pending_reset = False
            S.op("dve", lambda e, h3=h3: e.tensor_tensor(out=h3, in0=h3, in1=small[:, 3, :].unsqueeze(2).broadcast_to([128, 32, 64]), op=ALU.mult), reads=[("A", "hT")] + SK, writes=[("A", "hT")])
            S.op("dve", lambda e: e.tensor_tensor(out=hT, in0=hT, in1=self.bank(0, 4), op=ALU.add), reads=[("A", "hT")] + [("ps", t) for t in range(4)], writes=[("A", "hT")])
            seg_end = (tc % 2 == 1) if d == 0 else (tc % 2 == 0)
            if seg_end:
                seg = tc // 2
                S.op("sp", lambda e, seg=seg, d=d: e.dma_start(out=self.o_st[j, seg, d], in_=hT), reads=[("A", "hT")], dma=True, is_out=True)
                pending_reset = True
            YO = [("B", "yo", g_) for g_ in range(8)]
            if d == 0:
                S.op("sp", lambda e, tc=tc: e.dma_start(out=self.s_yf[tc], in_=ychb), reads=YO, writes=[("dr", "yf", tc)], dma=True)
            else:
                for fc in range(16):
                    bk = 6 + fc // 8
                    psb = self.PS[:, bk, :].bitcast(BF16)
                    S.op("pe", lambda e, fc=fc, psb=psb: e.transpose(out=psb[:, (fc % 8) * 128:(fc % 8 + 1) * 128], in_=yf[:, fc * 128:(fc + 1) * 128], identity=self.ident()),
                         reads=[("B", "yo", fc // 2)] + CB, writes=[("ps", bk)])
                do = CMAP["ssd_d"][0] + j * 16
                go = CMAP["ssd_ng"][0] + j * 16
                for hf in range(2):
                    fs = slice(hf * 8, (hf + 1) * 8)
                    psb = self.PS[:, 6 + hf, :].bitcast(BF16).rearrange("p (a b) -> p a b", a=8)
                    S.op("dve", lambda e, fs=fs, hf=hf: e.tensor_tensor(out=tmp8, in0=xbcT[:, fs, :], in1=self.consts[:, do + hf * 8: do + hf * 8 + 8].unsqueeze(2).broadcast_to([128, 8, 128]), op=ALU.mult),
                         reads=[XK] + CC, writes=[("B", "tmp8")])
                    S.op("dve", lambda e, psb=psb: e.tensor_tensor(out=tmp8, in0=tmp8, in1=psb, op=ALU.add), reads=[("B", "tmp8"), ("ps", 6 + hf)], writes=[("B", "tmp8")])
                    S.op("dve", lambda e, fs=fs: e.tensor_tensor(out=gat[:, fs, :], in0=tmp8, in1=zT[:, fs, :], op=ALU.mult), reads=[("B", "tmp8"), ("B", "zT")], writes=[("B", "gat", hf)])
                    S.op("act", lambda e, fs=fs: e.activation(out=sqb, in_=gat[:, fs, :], func=AF.Square), reads=[("B", "gat", hf)], writes=[("c", "sq", 0), ("c", "sq", 1)])
                    for f8 in range(8):
                        fc = hf * 8 + f8
                        S.op("pe", lambda e, fc=fc, f8=f8: e.matmul(self.PS[:, 4, 0:128], lhsT=self.ones(), rhs=sqb[:, f8, :], start=(fc == 0), stop=(fc == 15)), reads=[("c", "sq", 0), ("c", "sq", 1)] + CB, writes=[("ps", 4)])
                S.op("act", lambda e: e.activation(out=rs, in_=self.PS[:, 4, 0:128], func=AF.Sqrt, scale=1.0 / 2048, bias=EPS), reads=[("ps", 4)], writes=[("B", "rs")])
                S.op("dve", lambda e: e.reciprocal(out=rs, in_=rs), reads=[("B", "rs")], writes=[("B", "rs")])
                for hf in range(2):
                    fs = slice(hf * 8, (hf + 1) * 8)
                    S.op("dve", lambda e, fs=fs: e.tensor_tensor(out=tmp8, in0=gat[:, fs, :], in1=rs.unsqueeze(1).broadcast_to([128, 8, 128]), op=ALU.mult), reads=[("B", "gat", hf), ("B", "rs")], writes=[("B", "tmp8")])
                    S.op("dve", lambda e, fs=fs, hf=hf: e.tensor_tensor(out=gat[:, fs, :], in0=tmp8, in1=self.consts[:, go + hf * 8: go + hf * 8 + 8].unsqueeze(2).broadcast_to([128, 8, 128]), op=ALU.mult),
                         reads=[("B", "tmp8")] + CC, writes=[("B", "gat", hf)])
                S.op("sp", lambda e, tc=tc: e.dma_start(out=self.s_yn.rearrange("f p t -> p f t")[:, :, tc * 128:(tc + 1) * 128], in_=ynT), reads=[("B", "gat", 0), ("B", "gat", 1)], writes=[("dr", "yn", tc)], dma=True)
    TH = 1024
    for th in range(2):
        S.new_phase("A")
        S.new_phase("B")
        ar.reset()
        yh = ar.bf16(16, TH)
        wo = [ar.bf16(16, 256) for _ in range(2)]
        S.op("sp", lambda e, th=th: e.dma_start(out=yh, in_=self.s_yn.rearrange("f p t -> p f t")[:, :, th * TH:(th + 1) * TH]), reads=[("dr", "yn", t_) for t_ in range(16)], writes=[("A", "yh")], dma=True)
        for mt in range(4):
            b = mt % 2
            S.op(self.wq, lambda e, mt=mt, b=b: e.dma_start(out=wo[b], in_=self.d_sout[j, mt].rearrange("p (k n) -> p k n", k=16)), writes=[("B", "wo", b)], dma=True)
            for mc in range(2):
                m = mt * 2 + mc
                b0 = (m % 2) * 2
                for t2 in range(2):
                    for kc in range(16):
                        S.op("pe", lambda e, b=b, mc=mc, kc=kc, t2=t2, b0=b0: e.matmul(self.PS[:, b0 + t2, :], lhsT=wo[b][:, kc, mc * 128:(mc + 1) * 128], rhs=yh[:, kc, t2 * 512:(t2 + 1) * 512],
                                                                                        start=(kc == 0), stop=(kc == 15)),
                             reads=[("B", "wo", b), ("A", "yh")], writes=[("ps", b0 + t2)])
                S.op("act", lambda e, m=m, b0=b0, th=th: e.activation(out=self.hb[:, m, th * TH:(th + 1) * TH], in_=self.bank(b0, 2), func=AF.Identity),
                     reads=[("ps", b0), ("ps", b0 + 1)], writes=[("hb", m, 2 * th), ("hb", m, 2 * th + 1)])
    self.postnorm(l, 0)


Builder.mixer_ssd = mixer_ssd


def mixer_attn(self, l):
    import math
    S, ar = self.S, self.ar
    lam_init = 0.8 - 0.6 * math.exp(-0.3 * l)
    S.new_phase("A")
    S.new_phase("B")
    ar.reset()
    obuf = ar.bf16(KC, T)
    sinT = ar.f32(T)
    cosT = self.rstd
    markB = ar.off
    wt = ar.bf16(8, 384)
    qT = ar.bf16(T)
    kT2 = [ar.bf16(T + 256) for _ in range(2)]
    vtok = ar.bf16(18, 128)
    stg = [ar.f32(512) for _ in range(3)]
    vst = [ar.f32(512) for _ in range(2)]
    et = [self.sq[:, 0, :], self.sq[:, 1, :], ar.bf16(512), ar.bf16(512), ar.bf16(512)]
    etk = [("c", "sq", 0), ("c", "sq", 1), ("B", "et", 2), ("B", "et", 3), ("B", "et", 4)]
    sqt = ar.bf16(512)
    misc = self.misc
    CC = [("c", "consts")]
    self.prenorm(l, 0)
    S.op("dve", lambda e: e.memset(kT2[0][64:128, :], 0.0), writes=[("B", "kT", 0)])
    S.op("dve", lambda e: e.memset(kT2[1][0:64, :], 0.0), writes=[("B", "kT", 1)])
    S.op("sp", lambda e: e.dma_start(out=cosT[:, :], in_=self.d_rot[:, 0, :]), writes=[("c", "rstd", tt) for tt in range(4)], dma=True)
    S.op("sp", lambda e: e.dma_start(out=sinT, in_=self.d_rot[:, 1, :]), writes=[("A", "sin")], dma=True)
    lo = CMAP["lam"][0]
    lp = lambda i: self.consts[:, lo + i * 64: lo + (i + 1) * 64]
    for i in range(2):
        S.op("dve", lambda e, i=i: e.tensor_tensor(out=stg[0][:, i * 64:(i + 1) * 64], in0=lp(2 * i), in1=lp(2 * i + 1), op=ALU.mult), reads=CC, writes=[("B", "stg", 0)])
        S.op("dve", lambda e, i=i: e.reduce_sum(out=misc[:, i:i + 1], in_=stg[0][:, i * 64:(i + 1) * 64], axis=mybir.AxisListType.X), reads=[("B", "stg", 0)], writes=[("c", "misc")])
    S.op("act", lambda e: e.activation(out=misc[:, 2:4], in_=misc[:, 0:2], func=AF.Exp), reads=[("c", "misc")], writes=[("c", "misc")])
    S.op("dve", lambda e: e.tensor_tensor(out=misc[:, 4:5], in0=misc[:, 3:4], in1=misc[:, 2:3], op=ALU.subtract), reads=[("c", "misc")], writes=[("c", "misc")])
    S.op("dve", lambda e: e.tensor_scalar(out=misc[:, 4:5], in0=misc[:, 4:5], scalar1=-lam_init, scalar2=None, op0=ALU.add), reads=[("c", "misc")], writes=[("c", "misc")])
    S.op("dve", lambda e: e.tensor_scalar(out=misc[:, 5:6], in0=self.c("subg"), scalar1=1.0 - lam_init, scalar2=None, op0=ALU.mult), reads=CC + [("c", "misc")], writes=[("c", "misc")])
    neglam = misc[:, 4:5]
    gsub = misc[:, 5:6]
    ao = CMAP["abias"][0]
    rotp = self.c("rotp")
    si = [0]
    ei = [0]

    def proj_fm(colblk):
        for tt in range(4):
            for kc in range(KC):
                S.op("pe", lambda e, kc=kc, tt=tt: e.matmul(self.PS[:, tt, :], lhsT=wt[:, kc, colblk * 128:(colblk + 1) * 128],
                                                             rhs=self.hb[:, kc, tt * 512:(tt + 1) * 512], start=(kc == 0), stop=(kc == KC - 1)),
                     reads=[("B", "wt"), ("hb", kc, tt)], writes=[("ps", tt)])

    def rotary(dst, dst_off, hp, is_k):
        for tt in range(4):
            ts = slice(tt * 512, (tt + 1) * 512)
            i = si[0] % 3
            si[0] += 1
            j = si[0] % 3
            si[0] += 1
            pb = 4 + (tt % 2)
            S.op("act", lambda e, tt=tt, i=i: e.activation(out=stg[i], in_=self.PS[:, tt, :], func=AF.Identity), reads=[("ps", tt)], writes=[("B", "stg", i)])
            if is_k:
                S.op("sp", lambda e, tt=tt, i=i, ts=ts: e.dma_start(out=self.o_k[:, hp, ts], in_=stg[i]), reads=[("B", "stg", i)], dma=True, is_out=True)
            S.op("pe", lambda e, i=i, pb=pb: e.matmul(self.PS[:, pb, :], lhsT=rotp, rhs=stg[i], start=True, stop=True),
                 reads=[("B", "stg", i)] + CC, writes=[("ps", pb)])
            S.op("dve", lambda e, j=j, pb=pb, ts=ts: e.tensor_tensor(out=stg[j], in0=self.PS[:, pb, :], in1=sinT[:, ts], op=ALU.mult),
                 reads=[("ps", pb), ("A", "sin")], writes=[("B", "stg", j)])
            S.op("dve", lambda e, i=i, ts=ts, tt=tt: e.tensor_tensor(out=stg[i], in0=stg[i], in1=cosT[:, ts], op=ALU.mult),
                 reads=[("B", "stg", i), ("c", "rstd", tt)], writes=[("B", "stg", i)])
            for (dap, rows, dkey) in dst:
                S.op("dve", lambda e, i=i, j=j, tt=tt, dap=dap, rows=rows: e.tensor_tensor(out=dap[rows, dst_off + tt * 512: dst_off + (tt + 1) * 512], in0=stg[i][rows, :], in1=stg[j][rows, :], op=ALU.add),
                     reads=[("B", "stg", i), ("B", "stg", j)], writes=[dkey])

    for hp in range(8):
        S.op(self.wq, lambda e, hp=hp: e.dma_start(out=wt, in_=self.d_qkv[hp].rearrange("p (k n) -> p k n", k=8)), writes=[("B", "wt")], dma=True)
        S.op(self.wq, lambda e, hp=hp: e.dma_start(out=kT2[0][0:64, 0:256], in_=self.d_kc[0:64, hp, :]), writes=[("B", "kT", 0)], dma=True)
        S.op(self.wq, lambda e, hp=hp: e.dma_start(out=kT2[1][64:128, 0:256], in_=self.d_kc[64:128, hp, :]), writes=[("B", "kT", 1)], dma=True)
        S.op(self.wq, lambda e, hp=hp: e.dma_start(out=vtok[:, 0:2, :], in_=self.d_vc[:, :, hp * 128:(hp + 1) * 128]), writes=[("B", "vtok")], dma=True)
        proj_fm(0)
        rotary([(qT, slice(0, 128), ("B", "qT"))], 0, hp, False)
        proj_fm(1)
        rotary([(kT2[0], slice(0, 64), ("B", "kT", 0)), (kT2[1], slice(64, 128), ("B", "kT", 1))], 256, hp, True)
        for blk in range(16):
            bk, off = blk // 4, (blk % 4) * 128
            for kc in range(KC):
                S.op("pe", lambda e, kc=kc, blk=blk, bk=bk, off=off: e.matmul(self.PS[:, bk, off:off + 128], lhsT=self.hb[:, kc, blk * 128:(blk + 1) * 128],
                                                                              rhs=wt[:, kc, 256:384], start=(kc == 0), stop=(kc == KC - 1)),
                     reads=[("B", "wt"), ("hb", kc, blk // 4)], writes=[("ps", bk)])
        for bk in range(4):
            i = bk % 2
            S.op("act", lambda e, bk=bk, i=i: e.activation(out=vst[i], in_=self.PS[:, bk, :], func=AF.Identity), reads=[("ps", bk)], writes=[("B", "vst", i)])
            S.op("sp", lambda e, bk=bk, i=i, hp=hp: e.dma_start(out=self.o_v[:, 4 * bk:4 * bk + 4, hp * 128:(hp + 1) * 128], in_=vst[i].rearrange("p (a b) -> p a b", a=4)),
                 reads=[("B", "vst", i)], dma=True, is_out=True)
            S.op("dve", lambda e, bk=bk, i=i: e.tensor_copy(out=vtok[:, 2 + 4 * bk: 6 + 4 * bk, :], in_=vst[i].rearrange("p (a b) -> p a b", a=4)),
                 reads=[("B", "vst", i)], writes=[("B", "vtok")])
        items = [(qt, kb, hh) for qt in range(4) for kb in range(18) for hh in range(2)]
        LA = 3
        bA, bB, bC = stg
        kA, kB, kC = [("B", "stg", i) for i in range(3)]

        def emit_s(i, hp=hp):
            qt, kb, hh = items[i]
            qs = slice(qt * 512, (qt + 1) * 512)
            rows = slice(hh * 64, (hh + 1) * 64)
            sb, eb = i % 4, i % 5
            S.op("pe", lambda e: e.matmul(self.PS[:, sb, :], lhsT=kT2[hh][:, kb * 128:(kb + 1) * 128], rhs=qT[:, qs], start=True, stop=True),
                 reads=[("B", "kT", hh), ("B", "qT")], writes=[("ps", sb)])
            col = ao + kb * 8 + qt * 2
            ps3 = self.PS[:, sb, :].rearrange("p (a b) -> p a b", a=2)
            S.op("dve", lambda e: e.tensor_tensor(out=ps3, in0=ps3, in1=self.consts[:, col:col + 2].unsqueeze(2).broadcast_to([128, 2, 256]), op=ALU.add),
                 reads=[("ps", sb)] + CC, writes=[("ps", sb)])
            S.op("act", lambda e: e.activation(out=et[eb], in_=self.PS[:, sb, :], func=AF.Exp, scale=0.125), reads=[("ps", sb)], writes=[etk[eb]])

        def emit_pv(i, hp=hp):
            qt, kb, hh = items[i]
            qs = slice(qt * 512, (qt + 1) * 512)
            eb = i % 5
            S.op("pe", lambda e: e.matmul(self.PS[:, 4 + hh, :], lhsT=vtok[:, kb, :], rhs=et[eb], start=(kb == 0), stop=(kb == 17)),
                 reads=[("B", "vtok"), etk[eb]], writes=[("ps", 4 + hh)])
            S.op("pe", lambda e: e.matmul(self.PS[:, 6 + hh, :], lhsT=self.ones(), rhs=et[eb], start=(kb == 0), stop=(kb == 17)),
                 reads=[("c", "cbf"), etk[eb]], writes=[("ps", 6 + hh)])
            if not (kb == 17 and hh == 1):
                return
            S.op("dve", lambda e: e.tensor_copy(out=bA, in_=self.PS[:, 6, :]), reads=[("ps", 6)], writes=[kA])
            S.op("dve", lambda e: e.tensor_copy(out=bB, in_=self.PS[:, 7, :]), reads=[("ps", 7)], writes=[kB])
            S.op("dve", lambda e: e.reciprocal(out=bA, in_=bA), reads=[kA], writes=[kA])
            S.op("dve", lambda e: e.reciprocal(out=bB, in_=bB), reads=[kB], writes=[kB])
            S.op("dve", lambda e: e.tensor_tensor(out=bA, in0=self.PS[:, 4, :], in1=bA, op=ALU.mult), reads=[("ps", 4), kA], writes=[kA])
            S.op("dve", lambda e: e.tensor_tensor(out=bB, in0=self.PS[:, 5, :], in1=bB, op=ALU.mult), reads=[("ps", 5), kB], writes=[kB])
            S.op("dve", lambda e: e.scalar_tensor_tensor(out=bA, in0=bB, scalar=neglam, in1=bA, op0=ALU.mult, op1=ALU.add), reads=[kA, kB, ("c", "misc")], writes=[kA])
            S.op("dve", lambda e: e.tensor_tensor(out=sqt, in0=bA, in1=bA, op=ALU.mult), reads=[kA], writes=[("B", "sqt")])
            S.op("pe", lambda e: e.matmul(self.PS[:, 6, :], lhsT=self.ones(), rhs=sqt, start=True, stop=True), reads=[("c", "cbf"), ("B", "sqt")], writes=[("ps", 6)])
            S.op("act", lambda e: e.activation(out=bC, in_=self.PS[:, 6, :], func=AF.Ln, scale=1.0 / 128, bias=EPS), reads=[("ps", 6)], writes=[kC])
            S.op("act", lambda e: e.activation(out=bC, in_=bC, func=AF.Exp, scale=-0.5), reads=[kC], writes=[kC])
            S.op("dve", lambda e: e.scalar_tensor_tensor(out=obuf[:, hp, qs], in0=bA, scalar=gsub, in1=bC, op0=ALU.mult, op1=ALU.mult),
                 reads=[kA, kC, ("c", "misc")], writes=[("A", "o", hp)])

        for i in range(len(items) + LA):
            if i < len(items):
                emit_s(i)
            if i >= LA:
                emit_pv(i - LA)
    S.new_phase("B")
    ar.reset(markB)
    wo = ar.bf16(8, 1024)
    S.op(self.wq, lambda e: e.dma_start(out=wo, in_=self.d_dout.rearrange("p (k n) -> p k n", k=8)), writes=[("B", "wo")], dma=True)
    for mc in range(KC):
        b0 = (mc % 2) * 4
        for tt in range(4):
            for kc in range(KC):
                S.op("pe", lambda e, kc=kc, tt=tt, mc=mc, b0=b0: e.matmul(self.PS[:, b0 + tt, :], lhsT=wo[:, kc, mc * 128:(mc + 1) * 128],
                                                                          rhs=obuf[:, kc, tt * 512:(tt + 1) * 512], start=(kc == 0), stop=(kc == KC - 1)),
                     reads=[("B", "wo"), ("A", "o", kc)], writes=[("ps", b0 + tt)])
        S.op("act", lambda e, mc=mc, b0=b0: e.activation(out=self.hb[:, mc, :], in_=self.bank(b0, 4), func=AF.Identity),
             reads=[("ps", b0 + t) for t in range(4)], writes=[("hb", mc, tt) for tt in range(4)])
    self.postnorm(l, 0)


Builder.mixer_attn = mixer_attn


def _fm(v, nch):
    return np.ascontiguousarray(np.asarray(v, np.float32).reshape(nch, 128).T)


def _tile_cols(w, col_lists):
    K = w.shape[0]
    kc = K // 128
    out = []
    for cols in col_lists:
        t = w[:, cols].reshape(kc, 128, len(cols)).transpose(1, 0, 2)
        out.append(t.reshape(128, kc * len(cols)))
    return np.ascontiguousarray(np.stack(out), dtype=np.float32)


def _shared_weights(inp):
    ar = np.arange
    sh = {}
    sh["wmod"] = np.stack([_tile_cols(inp["w_mod"][l], [ar(j * 512, (j + 1) * 512) for j in range(12)]) for l in range(4)])
    fup = []
    for l in range(4):
        cl = []
        for pg in range(11):
            cl.append(np.concatenate([ar(pg * 256, (pg + 1) * 256), FFN + ar(pg * 256, (pg + 1) * 256)]))
        fup.append(_tile_cols(inp["ffn_w_up"][l], cl))
    sh["fup"] = np.stack(fup)
    sh["fdn"] = np.stack([_tile_cols(inp["ffn_w_down"][l], [ar(m * 256, (m + 1) * 256) for m in range(4)]) for l in range(4)])
    sh["ssdin"] = np.stack([_tile_cols(inp["ssd_w_in"][j], [ar(i * 512, (i + 1) * 512) for i in range(12)]) for j in range(2)])
    sh["ssddt"] = np.stack([_tile_cols(inp["ssd_w_in"][j], [ar(6144, 6208)])[0] for j in range(2)])
    sh["ssdout"] = np.stack([_tile_cols(inp["ssd_w_out"][j], [ar(m * 256, (m + 1) * 256) for m in range(4)]) for j in range(2)])
    sh["scin"] = _tile_cols(inp["sc_w_in"][0], [np.concatenate([ar(i * 128, (i + 1) * 128), 1024 + ar(i * 128, (i + 1) * 128), 2048 + ar(i * 128, (i + 1) * 128)]) for i in range(8)])
    sh["scout"] = _tile_cols(inp["sc_w_out"][0], [ar(0, 1024)])[0]
    sh["qkv"] = _tile_cols(inp["da_w_qkv"][0], [np.concatenate([ar(i * 128, (i + 1) * 128), 1024 + ar(i * 128, (i + 1) * 128), 2048 + ar(i * 128, (i + 1) * 128)]) for i in range(8)])
    sh["daout"] = _tile_cols(inp["da_w_out"][0], [ar(0, 1024)])[0]
    return sh


def _const_mats():
    k = np.arange(128)[:, None]
    s = np.arange(128)[None, :]
    m = {}
    m["ident"] = (k == s).astype(np.float32)
    m["ones"] = np.ones((128, 128), np.float32)
    P = np.zeros((128, 128), np.float32)
    for hb_ in (0, 64):
        for i in range(32):
            P[hb_ + i + 32, hb_ + i] = -1.0
            P[hb_ + i, hb_ + i + 32] = 1.0
    m["rotp"] = P
    m["triU"] = (k <= s).astype(np.float32)
    m["triL"] = (k >= s).astype(np.float32)
    m["LF"] = (k > s).astype(np.float32)
    m["LB"] = (k < s).astype(np.float32)
    m["maskF"] = np.where(k <= s, 0.0, NEG).astype(np.float32)
    m["maskB"] = np.where(k >= s, 0.0, NEG).astype(np.float32)
    return m


def _consts(inp, cond, is_sample):
    C = np.zeros((128, NCONST), np.float32)

    def put(name, arr):
        off, n = CMAP[name]
        arr = np.asarray(arr, np.float32).reshape(128, n)
        C[:, off:off + n] = arr

    put("cond", _fm(cond, 8))
    put("flag", np.full((128, 1), 1.0 if is_sample else 0.0))
    put("flagm1", np.full((128, 1), 0.0 if is_sample else -1.0))
    put("bmod", np.concatenate([_fm(inp["b_mod"][l], 48) for l in range(4)], axis=1))
    put("normg", np.concatenate([_fm(inp["norm_g"][l, i], 8) for l in range(4) for i in range(4)], axis=1))
    cw = np.zeros((128, 3, NCW), np.float32)
    for tap in range(3):
        for l in range(4):
            cw[:, tap, CW_FFN + l * 44: CW_FFN + (l + 1) * 44] = _fm(inp["ffn_conv_w"][l, tap], 44)
        for j in range(2):
            cw[:, tap, CW_SSD + j * 32: CW_SSD + (j + 1) * 32] = _fm(inp["ssd_conv_w"][j, tap], 32)
        cw[:, tap, CW_SC: CW_SC + 8] = _fm(inp["sc_conv_w"][0, tap], 8)
    put("cw", cw)
    put("ssd_cb", np.concatenate([_fm(inp["ssd_conv_b"][j], 32) for j in range(2)], axis=1))
    dtb = np.zeros((128, 2), np.float32)
    alog = np.zeros((128, 2), np.float32)
    for j in range(2):
        dtb[:64, j] = np.asarray(inp["ssd_dt_bias"][j]).reshape(64)
        alog[:64, j] = np.asarray(inp["ssd_a_log"][j]).reshape(64)
    put("ssd_dtb", dtb)
    put("ssd_alog", alog)
    put("ssd_d", np.concatenate([_fm(np.repeat(np.asarray(inp["ssd_d"][j]), 64), 16) for j in range(2)], axis=1))
    put("ssd_ng", np.concatenate([_fm(inp["ssd_norm_g"][j], 16) for j in range(2)], axis=1))
    put("lam", np.broadcast_to(np.asarray(inp["da_lambda"][0]).reshape(1, 256), (128, 256)))
    put("subg", np.asarray(inp["da_subln_g"][0]).reshape(128, 1))
    ab = np.zeros((18, 8), np.float32)
    if not is_sample:
        ab[:] = NEG
        for kb in range(2, 18):
            ab[kb, (kb - 2) // 2] = 0.0
    put("abias", np.broadcast_to(ab.reshape(1, 144), (128, 144)))
    for k_, v_ in _const_mats().items():
        put(k_, v_)
    return C


def _rot_tables(is_sample):
    R = np.zeros((128, 2, T), np.float32)
    if not is_sample:
        R[:, 0, :] = 1.0
        return R
    rows = T // 64
    row = np.repeat(np.arange(rows, dtype=np.float32), 64)
    col = np.tile(np.arange(64, dtype=np.float32), rows)
    inv = (10000.0 ** (-np.arange(16, dtype=np.float32) / 16)).astype(np.float32)
    ang = np.concatenate([row[:, None] * inv, col[:, None] * inv], axis=-1).astype(np.float32)
    cos, sin = np.cos(ang).T, np.sin(ang).T
    for p in range(128):
        R[p, 0] = cos[(p % 64) % 32]
        R[p, 1] = sin[(p % 64) % 32]
    return R


def _core_inputs(inp, core, shared):
    is_sample = core < 4
    if is_sample:
        xs = np.asarray(inp["x_sample"][core], np.float32)
        cond = np.asarray(inp["c"][core])
        st = np.asarray(inp["state_ssm"][core], np.float32)
        ssm0 = np.ascontiguousarray(st.transpose(0, 1, 4, 2, 3).reshape(2, 2, 128, 2048))
        kc_ = np.asarray(inp["cache_k"][core, 0], np.float32).reshape(256, 8, 128)
        kcache = np.ascontiguousarray(kc_.transpose(2, 1, 0))
        vc_ = np.asarray(inp["cache_v"][core, 0], np.float32).reshape(2, 128, 1024)
        vcache = np.ascontiguousarray(vc_.transpose(1, 0, 2))
    else:
        s0 = (core - 4) * 8
        xs = np.asarray(inp["x_prompt"][s0:s0 + 8], np.float32).reshape(T, D)
        cond = np.asarray(inp["c_ctx"])
        ssm0 = np.zeros((2, 2, 128, 2048), np.float32)
        kcache = np.zeros((128, 8, 256), np.float32)
        vcache = np.zeros((128, 2, 1024), np.float32)
    d = dict(shared)
    d["xT"] = np.ascontiguousarray(xs.reshape(T, KC, 128).transpose(2, 1, 0))
    d["consts"] = _consts(inp, cond, is_sample)
    d["rot"] = _rot_tables(is_sample)
    d["ssm0"] = ssm0
    d["kcache"] = kcache
    d["vcache"] = vcache
    return d


_BUILD_CACHE = {}
_DEBUG = [False]


def _get_builder(layers, mixers=None):
    key = (tuple(layers), None if mixers is None else tuple(mixers))
    if key not in _BUILD_CACHE:
        b = Builder(layers, mixers)
        b.debug = _DEBUG[0]
        b.build()
        _BUILD_CACHE[key] = b
    return _BUILD_CACHE[key]


def run_cores(inputs, layers=(0, 1, 2, 3), mixers=None):
    inp = {k: np.asarray(v) for k, v in inputs.items()}
    shared = _shared_weights(inp)
    in_maps = [_core_inputs(inp, c, shared) for c in range(8)]
    b = _get_builder(layers, mixers)
    res = run_bass_kernel_spmd(b.nc, in_maps, core_ids=list(range(8)))
    return res.results


def kernel(**inputs):
    r = run_cores(inputs)
    B, SEQ = 32, 256
    y_s = np.stack([r[c]["yT"].transpose(2, 1, 0).reshape(T, D) for c in range(4)]).astype(np.float32)
    y_p = np.concatenate([r[c]["yT"].transpose(2, 1, 0).reshape(8, SEQ, D) for c in range(4, 8)]).astype(np.float32)
    st = np.concatenate([r[c]["stout"].reshape(2, 8, 2, 128, 32, 64).transpose(1, 0, 2, 4, 5, 3) for c in range(4, 8)]).astype(np.float32)
    kn = np.concatenate([r[c]["knew"].transpose(2, 1, 0).reshape(8, 1, SEQ, 16, 64) for c in range(4, 8)]).astype(np.float32)
    vn = np.concatenate([r[c]["vnew"].transpose(1, 0, 2).reshape(8, 1, SEQ, 8, 128) for c in range(4, 8)]).astype(np.float32)
    return (np.ascontiguousarray(y_p), np.ascontiguousarray(y_s), np.ascontiguousarray(st), np.ascontiguousarray(kn), np.ascontiguousarray(vn))
```

```python
import contextlib
import numpy as np
import concourse.bass as bass
import concourse.mybir as mybir
from concourse.bass_utils import run_bass_kernel_spmd

F32 = mybir.dt.float32
BF16 = mybir.dt.bfloat16
AF = mybir.ActivationFunctionType
ALU = mybir.AluOpType

T = 2048
D = 1024
KC = 8
NSEG = 8
SEGL = 256
DEPTH = 4
FFN = 2816
NPAIR = 22
EPS = 1e-6
NEG = -30000.0

ENGS = ("pe", "act", "dve", "pool", "sp")


class _Op:
    __slots__ = ("eng", "fn", "deps", "dma", "idx", "gidx", "inc", "cnt", "waits", "dsem", "dcnt", "dprev")

    def __init__(self, eng, fn, dma):
        self.eng = eng
        self.fn = fn
        self.dma = dma
        self.deps = set()
        self.inc = False
        self.cnt = 0
        self.waits = []
        self.dsem = None
        self.dcnt = 0
        self.dprev = None


class Sched:
    def __init__(self, nc, n_dma_sems=32, same_eng_dist=3):
        self.nc = nc
        self.ops = []
        self.per = {e: [] for e in ENGS}
        self.last_w = {}
        self.readers = {}
        self.region_deps = {}
        self.n_dma_sems = n_dma_sems
        self.same_eng_dist = same_eng_dist
        self.out_dmas = []

    def new_phase(self, region):
        s = set(self.region_deps.get(region, ()))
        for tab in (self.last_w, self.readers):
            for k in [k for k in tab if k[0] == region]:
                v = tab.pop(k)
                if isinstance(v, list):
                    s.update(v)
                else:
                    s.add(v)
        for r in (("A", "B") if region in ("A", "B") else (region,)):
            self.region_deps[r] = set(self.region_deps.get(r, ())) | s

    @staticmethod
    def _freeze(fn):
        import types
        if fn.__closure__ is None:
            return fn
        cells = tuple(types.CellType(c.cell_contents) for c in fn.__closure__)
        g = types.FunctionType(fn.__code__, fn.__globals__, fn.__name__, fn.__defaults__, cells)
        g.__kwdefaults__ = fn.__kwdefaults__
        return g

    def op(self, eng, fn, reads=(), writes=(), dma=False, is_out=False):
        fn = self._freeze(fn)
        o = _Op(eng, fn, dma)
        o.idx = len(self.per[eng])
        o.gidx = len(self.ops)
        for k in list(reads) + list(writes):
            assert isinstance(k, tuple), k
            if k[0] in self.region_deps and k not in self.last_w and k not in self.readers:
                o.deps.update(self.region_deps[k[0]])
        for k in reads:
            w = self.last_w.get(k)
            if w is not None:
                o.deps.add(w)
        for k in writes:
            w = self.last_w.get(k)
            if w is not None:
                o.deps.add(w)
            for r in self.readers.get(k, ()):
                o.deps.add(r)
        o.deps.discard(o)
        for k in reads:
            self.readers.setdefault(k, []).append(o)
        for k in writes:
            self.last_w[k] = o
            self.readers[k] = []
        self.per[eng].append(o)
        self.ops.append(o)
        if is_out:
            self.out_dmas.append(o)
        return o

    def _skip(self, o, d):
        if d.eng != o.eng:
            return False
        if d.eng == "pe":
            return True
        return (o.idx - d.idx) >= self.same_eng_dist

    def emit(self):
        nc = self.nc
        for o in self.ops:
            best = {}
            for d in o.deps:
                if d.dma or self._skip(o, d):
                    continue
                if d.eng not in best or best[d.eng].idx < d.idx:
                    best[d.eng] = d
            for d in best.values():
                d.inc = True
            o.deps = set(d for d in o.deps if d.dma) | set(best.values())
        fin = _Op("sp", lambda e: e.nop(), False)
        fin.idx = len(self.per["sp"])
        fin.gidx = len(self.ops)
        for o in self.out_dmas:
            fin.deps.add(o)
        for e in ENGS:
            if e != "sp" and self.per[e]:
                last = self.per[e][-1]
                if not last.dma:
                    last.inc = True
                fin.deps.add(last)
        self.per["sp"].append(fin)
        self.ops.append(fin)
        cnt = {e: 0 for e in ENGS}
        dma_n = [0] * self.n_dma_sems
        dma_last = [None] * self.n_dma_sems
        rr = 0
        for o in self.ops:
            if o.dma:
                j = rr % self.n_dma_sems
                rr += 1
                o.dsem = j
                dma_n[j] += 16
                o.dcnt = dma_n[j]
                o.dprev = dma_last[j]
                dma_last[j] = o
            elif o.inc:
                cnt[o.eng] += 1
                o.cnt = cnt[o.eng]
        known = {e: {} for e in ENGS}
        snap = {}
        for o in self.ops:
            kn = known[o.eng]
            need = {}
            deps = list(o.deps)
            if o.dma and o.dprev is not None:
                deps.append(o.dprev)
            for d in deps:
                if d.dma:
                    key, val = ("d", d.dsem), d.dcnt
                else:
                    if not d.inc or self._skip(o, d):
                        continue
                    key, val = d.eng, d.cnt
                if kn.get(key, 0) >= val:
                    continue
                if need.get(key, 0) < val:
                    need[key] = val
            for key, val in need.items():
                if kn.get(key, 0) < val:
                    kn[key] = val
                s = snap.get((key, val))
                if s is not None:
                    for k2, v2 in s.items():
                        if kn.get(k2, 0) < v2:
                            kn[k2] = v2
            o.waits = list(need.items())
            if o.dma:
                snap[(("d", o.dsem), o.dcnt)] = dict(kn)
            elif o.inc:
                snap[(o.eng, o.cnt)] = dict(kn)
        self.stats = {e: (len(self.per[e]), cnt[e]) for e in ENGS}
        self.stats["dma"] = rr
        with contextlib.ExitStack() as st:
            esem = {e: st.enter_context(nc.semaphore("s_" + e)) for e in ENGS}
            dsem = [st.enter_context(nc.semaphore("d%d" % j)) for j in range(self.n_dma_sems)]
            block = st.enter_context(nc.Block())

            def run(eng_name):
                def body(e):
                    for o in self.per[eng_name]:
                        for key, val in o.waits:
                            s = dsem[key[1]] if isinstance(key, tuple) else esem[key]
                            e.wait_ge(s, val)
                        ins = o.fn(e)
                        if o.dma:
                            ins.then_inc(dsem[o.dsem], 16)
                        elif o.inc:
                            ins.then_inc(esem[eng_name], 1)
                return body

            block.tensor(run("pe"))
            block.scalar(run("act"))
            block.vector(run("dve"))
            block.gpsimd(run("pool"))
            block.sync(run("sp"))


def _prod(s):
    n = 1
    for v in s:
        n *= v
    return n


class Arena:
    def __init__(self, ap32, nwords):
        self.ap = ap32
        self.nw = nwords
        self.off = 0

    def reset(self, off=0):
        self.off = off

    def _take(self, words):
        a = self.off
        self.off += words
        assert self.off <= self.nw, ("arena overflow", self.off, self.nw)
        return self.ap[:, a:a + words]

    @staticmethod
    def _shape(ap, shape):
        if len(shape) == 1:
            return ap
        if len(shape) == 2:
            return ap.rearrange("p (a b) -> p a b", a=shape[0])
        return ap.rearrange("p (a b c) -> p a b c", a=shape[0], b=shape[1])

    def f32(self, *shape):
        return self._shape(self._take(_prod(shape)), shape)

    def bf16(self, *shape):
        n = _prod(shape)
        assert n % 2 == 0
        return self._shape(self._take(n // 2).bitcast(BF16), shape)


def _cmap():
    m = {}
    off = 0
    for name, n in (("cond", 8), ("flag", 1), ("flagm1", 1), ("bmod", 4 * 48), ("normg", 16 * 8),
                    ("cw", 3 * 248), ("ssd_cb", 2 * 32), ("ssd_dtb", 2), ("ssd_alog", 2), ("ssd_d", 2 * 16),
                    ("ssd_ng", 2 * 16), ("lam", 4 * 64), ("subg", 1), ("abias", 18 * 8),
                    ("ident", 128), ("ones", 128), ("rotp", 128), ("triU", 128), ("triL", 128),
                    ("LF", 128), ("LB", 128), ("maskF", 128), ("maskB", 128)):
        m[name] = (off, n)
        off += n
    return m, off


CMAP, NCONST = _cmap()
NCW = 248
CW_FFN, CW_SSD, CW_SC = 0, 176, 240
ARENA_WORDS = 19968


class Builder:
    def __init__(self, layers=(0, 1, 2, 3), mixers=None):
        self.layers = tuple(layers)
        self.mixers = set(layers) if mixers is None else set(mixers)
        self.nc = nc = bass.Bass("TRN2", target_bir_lowering=False)
        import os as _os
        self.sim = _os.environ.get("KSIM", "0") == "1"
        self.wq = "sp" if self.sim else "pool"
        WDT = BF16 if self.sim else F32
        self.S = Sched(nc)
        self.st = contextlib.ExitStack()

        def din(name, shape, dt=F32):
            return nc.dram_tensor(name, list(shape), dt, kind="ExternalInput").ap()

        def dout(name, shape, dt=F32):
            return nc.dram_tensor(name, list(shape), dt, kind="ExternalOutput").ap()

        def dscr(name, shape, dt):
            return nc.dram_tensor(name, list(shape), dt, kind="Internal").ap()

        self.d_x = din("xT", [128, KC, T])
        self.d_consts = din("consts", [128, NCONST])
        self.d_rot = din("rot", [128, 2, T])
        self.d_ssm0 = din("ssm0", [2, 2, 128, 2048])
        self.d_kc = din("kcache", [128, 8, 256], WDT)
        self.d_vc = din("vcache", [128, 2, 1024], WDT)
        self.d_wmod = din("wmod", [4, 12, 128, 8 * 512], WDT)
        self.d_fup = din("fup", [4, 11, 128, 8 * 512], WDT)
        self.d_fdn = din("fdn", [4, 4, 128, 22 * 256], WDT)
        self.d_sin = din("ssdin", [2, 12, 128, 8 * 512], WDT)
        self.d_sdt = din("ssddt", [2, 128, 8 * 64], WDT)
        self.d_sout = din("ssdout", [2, 4, 128, 16 * 256], WDT)
        self.d_scin = din("scin", [8, 128, 8 * 384], WDT)
        self.d_scout = din("scout", [128, 8 * 1024], WDT)
        self.d_qkv = din("qkv", [8, 128, 8 * 384], WDT)
        self.d_dout = din("daout", [128, 8 * 1024], WDT)
        self.o_y = dout("yT", [128, KC, T])
        self.o_st = dout("stout", [2, NSEG, 2, 128, 2048])
        self.o_k = dout("knew", [128, KC, T])
        self.o_v = dout("vnew", [128, 16, 1024])
        self.s_xbc = dscr("s_xbc", [16, 128, 32 * 128], BF16)
        self.s_z = dscr("s_z", [16, 128, 16 * 128], BF16)
        self.s_yf = dscr("s_yf", [16, 128, 2048], BF16)
        self.s_yn = dscr("s_yn", [16, 128, T], BF16)

        sb = lambda name, shape, dt: self.st.enter_context(nc.sbuf_tensor(name, list(shape), dt))
        self.xres = sb("xres", [128, KC, T], F32)
        self.hb = sb("hb", [128, KC, T], BF16)
        self.consts = sb("consts_sb", [128, NCONST], F32)
        self.rstd = sb("rstd", [128, T], F32)
        self.mod = sb("mod", [128, 4, 48], F32)
        self.der = sb("der", [128, 4, 4, 8], F32)
        self.cwd = sb("cwd", [128, 4, NCW], F32)
        self.scond = sb("scond", [128, 8], BF16)
        self.cbf = sb("cbf", [128, 9, 128], BF16)
        self.sq = sb("sq", [128, 2, 512], BF16)
        self.tmpf = sb("tmpf", [128, 2, 512], F32)
        self.misc = sb("misc", [128, 16], F32)
        self.hh = sb("hh", [128, KC, 2], BF16)
        arena_t = sb("arena", [128, ARENA_WORDS], F32)
        self.ar = Arena(arena_t[:, :], ARENA_WORDS)
        self.PS = self.st.enter_context(nc.psum_tensor("PS", [128, 8, 512], F32))
        self.sqi = 0
        self.tmi = 0

    def dbg(self, name, ap, keys, dt=F32):
        if not getattr(self, "debug", False):
            return
        shape = [int(v) for v in ap.shape]
        d = self.nc.dram_tensor("dbg_" + name, shape, dt, kind="ExternalOutput").ap()
        self.S.op("sp", lambda e: e.dma_start(out=d, in_=ap), reads=list(keys), dma=True, is_out=True)

    def c(self, name, a=None, b=None):
        off, n = CMAP[name]
        if a is None:
            return self.consts[:, off:off + n]
        return self.consts[:, off + a:off + (a + 1 if b is None else b)]

    def cw(self, tap, col):
        off, _ = CMAP["cw"]
        return self.consts[:, off + tap * NCW + col: off + tap * NCW + col + 1]

    def cwdv(self, which, col):
        return self.cwd[:, which, col:col + 1]

    def bank(self, b, n=1):
        return self.PS[:, b:b + n, :].rearrange("p b n -> p (b n)") if n > 1 else self.PS[:, b, :]

    def setup(self):
        S = self.S
        S.op("sp", lambda e: e.dma_start(out=self.consts[:, :], in_=self.d_consts), writes=[("c", "consts")], dma=True)
        for kc in range(KC):
            S.op("sp", lambda e, kc=kc: e.dma_start(out=self.xres[:, kc, :], in_=self.d_x[:, kc, :]),
                 writes=[("x", kc, tt) for tt in range(4)], dma=True)
        CC = [("c", "consts")]
        S.op("act", lambda e: e.activation(out=self.scond[:, :], in_=self.c("cond"), func=AF.Silu), reads=CC, writes=[("c", "scond")])
        for i, nm in enumerate(("ident", "ones", "triU", "triL", "LF", "LB", "maskF", "maskB")):
            S.op("dve", lambda e, i=i, nm=nm: e.tensor_copy(out=self.cbf[:, i, :], in_=self.c(nm)), reads=CC, writes=[("c", "cbf")])
        off = CMAP["cw"][0]
        cwall = self.consts[:, off:off + 3 * NCW].rearrange("p (a n) -> p a n", a=3)
        for w, (tap, fl) in enumerate(((0, "flag"), (2, "flag"), (0, "flagm1"), (2, "flagm1"))):
            S.op("dve", lambda e, w=w, tap=tap, fl=fl: e.tensor_scalar(out=self.cwd[:, w, :], in0=cwall[:, tap, :], scalar1=self.c(fl), scalar2=None, op0=ALU.mult),
                 reads=CC, writes=[("c", "cwd")])

    def ident(self):
        return self.cbf[:, 0, :]

    def ones(self):
        return self.cbf[:, 1, :]

    def phase_mod(self):
        S, ar = self.S, self.ar
        S.new_phase("A")
        ar.reset()
        wb = [ar.bf16(8, 512) for _ in range(4)]
        psm = self.PS[:, 7, :]
        for l in self.layers:
            for j in range(12):
                b = j % 4
                S.op(self.wq, lambda e, l=l, j=j, b=b: e.dma_start(out=wb[b], in_=self.d_wmod[l, j].rearrange("p (k n) -> p k n", k=8)),
                     writes=[("A", "w", b)], dma=True)
                for m in range(4):
                    col = l * 48 + j * 4 + m
                    for kc in range(KC):
                        S.op("pe", lambda e, b=b, m=m, kc=kc, col=col: e.matmul(psm[:, col:col + 1], lhsT=wb[b][:, kc, m * 128:(m + 1) * 128],
                                                                                 rhs=self.scond[:, kc:kc + 1], start=(kc == 0), stop=(kc == KC - 1)),
                             reads=[("A", "w", b), ("c", "scond")], writes=[("ps", 7)])
            bo = CMAP["bmod"][0]
            S.op("dve", lambda e, l=l, bo=bo: e.tensor_tensor(out=self.mod[:, l, :], in0=psm[:, l * 48:(l + 1) * 48],
                                                              in1=self.consts[:, bo + l * 48: bo + (l + 1) * 48], op=ALU.add),
                 reads=[("ps", 7), ("c", "consts")], writes=[("c", "mod", l)])
            go = CMAP["normg"][0]
            g = lambda i, l=l: self.consts[:, go + (l * 4 + i) * 8: go + (l * 4 + i + 1) * 8]
            mo = lambda i, l=l: self.mod[:, l, i * 8:(i + 1) * 8]
            for di, (mi, gi, plus1) in enumerate(((1, 0, True), (2, 1, False), (4, 2, True), (5, 3, False))):
                if plus1:
                    S.op("dve", lambda e, l=l, di=di, mi=mi, gi=gi: e.scalar_tensor_tensor(out=self.der[:, l, di, :], in0=mo(mi), scalar=1.0, in1=g(gi), op0=ALU.add, op1=ALU.mult),
                         reads=[("c", "mod", l), ("c", "consts")], writes=[("c", "der", l, di)])
                else:
                    S.op("dve", lambda e, l=l, di=di, mi=mi, gi=gi: e.tensor_tensor(out=self.der[:, l, di, :], in0=mo(mi), in1=g(gi), op=ALU.mult),
                         reads=[("c", "mod", l), ("c", "consts")], writes=[("c", "der", l, di)])

    def shift(self, l, which):
        return self.mod[:, l, which * 8:(which + 1) * 8]

    def _sumsq_rstd(self, src_ap_fn, src_keys_fn, tt, bankno, on_dve=False):
        S = self.S
        ts = slice(tt * 512, (tt + 1) * 512)
        for kc in range(KC):
            i = self.sqi % 2
            self.sqi += 1
            if on_dve and kc % 2 == 1:
                S.op("dve", lambda e, kc=kc, i=i: e.tensor_tensor(out=self.sq[:, i, :], in0=src_ap_fn(kc, ts), in1=src_ap_fn(kc, ts), op=ALU.mult),
                     reads=src_keys_fn(kc, tt), writes=[("c", "sq", i)])
            else:
                S.op("act", lambda e, kc=kc, i=i: e.activation(out=self.sq[:, i, :], in_=src_ap_fn(kc, ts), func=AF.Square),
                     reads=src_keys_fn(kc, tt), writes=[("c", "sq", i)])
            S.op("pe", lambda e, kc=kc, i=i: e.matmul(self.PS[:, bankno, :], lhsT=self.ones(), rhs=self.sq[:, i, :], start=(kc == 0), stop=(kc == KC - 1)),
                 reads=[("c", "sq", i), ("c", "cbf")], writes=[("ps", bankno)])
        S.op("act", lambda e: e.activation(out=self.rstd[:, ts], in_=self.PS[:, bankno, :], func=AF.Sqrt, scale=1.0 / D, bias=EPS),
             reads=[("ps", bankno)], writes=[("c", "rstd", tt)])
        S.op("dve", lambda e: e.reciprocal(out=self.rstd[:, ts], in_=self.rstd[:, ts]), reads=[("c", "rstd", tt)], writes=[("c", "rstd", tt)])

    def prenorm(self, l, which, tts=range(4)):
        S = self.S
        di = 0 if which == 0 else 2
        shi = 0 if which == 0 else 3
        for tt in tts:
            ts = slice(tt * 512, (tt + 1) * 512)
            bankno = 6 + (tt % 2)
            self._sumsq_rstd(lambda kc, ts: self.xres[:, kc, ts], lambda kc, tt: [("x", kc, tt)], tt, bankno, on_dve=False)
            for kc in range(KC):
                i = self.tmi % 2
                self.tmi += 1
                S.op("dve", lambda e, kc=kc, i=i, ts=ts: e.scalar_tensor_tensor(out=self.tmpf[:, i, :], in0=self.xres[:, kc, ts], scalar=self.der[:, l, di, kc:kc + 1],
                                                                                  in1=self.rstd[:, ts], op0=ALU.mult, op1=ALU.mult),
                     reads=[("x", kc, tt), ("c", "der", l, di), ("c", "rstd", tt)], writes=[("c", "tmpf", i)])
                S.op("act", lambda e, kc=kc, i=i, ts=ts: e.activation(out=self.hb[:, kc, ts], in_=self.tmpf[:, i, :], func=AF.Identity,
                                                                       bias=self.mod[:, l, shi * 8 + kc: shi * 8 + kc + 1], scale=1.0),
                     reads=[("c", "tmpf", i), ("c", "mod", l)], writes=[("hb", kc, tt)])

    def postnorm(self, l, which, tts=range(4)):
        S = self.S
        di = 1 if which == 0 else 3
        for tt in tts:
            ts = slice(tt * 512, (tt + 1) * 512)
            bankno = 6 + (tt % 2)
            self._sumsq_rstd(lambda kc, ts: self.hb[:, kc, ts], lambda kc, tt: [("hb", kc, tt)], tt, bankno)
            for kc in range(KC):
                i = self.tmi % 2
                self.tmi += 1
                S.op("dve", lambda e, kc=kc, i=i, ts=ts: e.scalar_tensor_tensor(out=self.tmpf[:, i, :], in0=self.hb[:, kc, ts], scalar=self.der[:, l, di, kc:kc + 1],
                                                                                  in1=self.rstd[:, ts], op0=ALU.mult, op1=ALU.mult),
                     reads=[("hb", kc, tt), ("c", "der", l, di), ("c", "rstd", tt)], writes=[("c", "tmpf", i)])
                S.op("dve", lambda e, kc=kc, i=i, ts=ts: e.tensor_tensor(out=self.xres[:, kc, ts], in0=self.tmpf[:, i, :], in1=self.xres[:, kc, ts], op=ALU.add),
                     reads=[("c", "tmpf", i), ("x", kc, tt)], writes=[("x", kc, tt)])

    def conv3(self, src, src_keys, acc, acc_key, ccol, n, halo_l=None, halo_r=None, halo_keys=()):
        S = self.S
        R = list(src_keys) + [("c", "consts"), ("c", "cwd")]
        W = [acc_key]
        S.op("act", lambda e: e.activation(out=acc[:, 0:n], in_=src[:, 0:n], func=AF.Identity, scale=self.cw(1, ccol)), reads=R, writes=W)
        S.op("dve", lambda e: e.scalar_tensor_tensor(out=acc[:, 1:n], in0=src[:, 0:n - 1], scalar=self.cw(0, ccol), in1=acc[:, 1:n], op0=ALU.mult, op1=ALU.add),
             reads=R + W, writes=W)
        S.op("dve", lambda e: e.scalar_tensor_tensor(out=acc[:, 0:n - 1], in0=src[:, 1:n], scalar=self.cw(2, ccol), in1=acc[:, 0:n - 1], op0=ALU.mult, op1=ALU.add),
             reads=R + W, writes=W)
        nb = n // SEGL - 1
        S.op("dve", lambda e: e.scalar_tensor_tensor(out=acc[:, SEGL:n:SEGL], in0=src[:, SEGL - 1:n - 1:SEGL], scalar=self.cwdv(2, ccol), in1=acc[:, SEGL:n:SEGL], op0=ALU.mult, op1=ALU.add),
             reads=R + W, writes=W)
        S.op("dve", lambda e: e.scalar_tensor_tensor(out=acc[:, SEGL - 1:n - 1:SEGL], in0=src[:, SEGL:n:SEGL], scalar=self.cwdv(3, ccol), in1=acc[:, SEGL - 1:n - 1:SEGL], op0=ALU.mult, op1=ALU.add),
             reads=R + W, writes=W)
        if halo_l is not None:
            S.op("dve", lambda e: e.scalar_tensor_tensor(out=acc[:, 0:1], in0=halo_l, scalar=self.cwdv(0, ccol), in1=acc[:, 0:1], op0=ALU.mult, op1=ALU.add),
                 reads=R + W + list(halo_keys), writes=W)
        if halo_r is not None:
            S.op("dve", lambda e: e.scalar_tensor_tensor(out=acc[:, n - 1:n], in0=halo_r, scalar=self.cwdv(1, ccol), in1=acc[:, n - 1:n], op0=ALU.mult, op1=ALU.add),
                 reads=R + W + list(halo_keys), writes=W)

    def ffn(self, l):
        S, ar = self.S, self.ar
        TH = 1024
        self.prenorm(l, 1)
        S.op("dve", lambda e: e.tensor_copy(out=self.hh[:, :, :], in_=self.hb[:, :, TH - 1:TH + 1]),
             reads=[("hb", kc, tt) for kc in range(KC) for tt in (1, 2)], writes=[("c", "hh")])
        self.dbg("mod%d" % l, self.mod[:, l, :], [("c", "mod", l)])
        self.dbg("der%d" % l, self.der[:, l, :, :], [("c", "der", l, i) for i in range(4)])
        self.dbg("h%d" % l, self.hb[:, :, 0:512], [("hb", kc, 0) for kc in range(KC)], BF16)
        self.dbg("rstd%d" % l, self.rstd[:, 0:512], [("c", "rstd", 0)])
        for th in range(2):
            S.new_phase("A")
            S.new_phase("B")
            ar.reset()
            hid = ar.bf16(NPAIR, TH)
            markB = ar.off
            wu = [ar.bf16(8, 512) for _ in range(2)]
            accg = [ar.f32(TH) for _ in range(2)]
            accv = [ar.f32(TH) for _ in range(2)]
            hcol = 1 if th == 0 else 0
            ci = 0
            for pg in range(11):
                b = pg % 2
                S.op(self.wq, lambda e, pg=pg, b=b: e.dma_start(out=wu[b], in_=self.d_fup[l, pg].rearrange("p (k n) -> p k n", k=8)),
                     writes=[("B", "wu", b)], dma=True)
                for pi in range(2):
                    for isval in (0, 1):
                        c = pi + 2 * isval
                        hf = ci % 2
                        ci += 1
                        b0 = hf * 3
                        for t2 in range(2):
                            for kc in range(KC):
                                S.op("pe", lambda e, b=b, c=c, kc=kc, t2=t2, b0=b0: e.matmul(self.PS[:, b0 + t2, :], lhsT=wu[b][:, kc, c * 128:(c + 1) * 128],
                                                                                              rhs=self.hb[:, kc, th * TH + t2 * 512: th * TH + (t2 + 1) * 512],
                                                                                              start=(kc == 0), stop=(kc == KC - 1)),
                                     reads=[("B", "wu", b), ("hb", kc, 2 * th + t2)], writes=[("ps", b0 + t2)])
                        for kc in range(KC):
                            S.op("pe", lambda e, b=b, c=c, kc=kc, b0=b0: e.matmul(self.PS[:, b0 + 2, 0:1], lhsT=wu[b][:, kc, c * 128:(c + 1) * 128],
                                                                                   rhs=self.hh[:, kc, hcol:hcol + 1], start=(kc == 0), stop=(kc == KC - 1)),
                                 reads=[("B", "wu", b), ("c", "hh")], writes=[("ps", b0 + 2)])
                        ccol = CW_FFN + l * 44 + (22 if isval else 0) + 2 * pg + pi
                        acc = (accv if isval else accg)[pi]
                        akey = ("B", "accv" if isval else "accg", pi)
                        src = self.bank(b0, 2)
                        halo = self.PS[:, b0 + 2, 0:1]
                        self.conv3(src, [("ps", b0), ("ps", b0 + 1)], acc, akey, ccol, TH,
                                   halo_l=(halo if th == 1 else None), halo_r=(halo if th == 0 else None), halo_keys=[("ps", b0 + 2)])
                    j = 2 * pg + pi
                    S.op("act", lambda e, pi=pi: e.activation(out=accg[pi], in_=accg[pi], func=AF.Silu), reads=[("B", "accg", pi)], writes=[("B", "accg", pi)])
                    S.op("dve", lambda e, pi=pi, j=j: e.tensor_tensor(out=hid[:, j, :], in0=accg[pi], in1=accv[pi], op=ALU.mult),
                         reads=[("B", "accg", pi), ("B", "accv", pi)], writes=[("A", "hid", j)])
            if th == 0:
                self.dbg("hid%d" % l, hid[:, 0:2, :], [("A", "hid", 0), ("A", "hid", 1)], BF16)
                self.dbg("accv%d" % l, accv[0], [("B", "accv", 0)])
            S.new_phase("B")
            ar.reset(markB)
            wd = [ar.bf16(NPAIR, 256) for _ in range(2)]
            for mt in range(4):
                b = mt % 2
                S.op(self.wq, lambda e, mt=mt, b=b: e.dma_start(out=wd[b], in_=self.d_fdn[l, mt].rearrange("p (k n) -> p k n", k=NPAIR)),
                     writes=[("B", "wd", b)], dma=True)
                for mc in range(2):
                    m = mt * 2 + mc
                    b0 = (m % 2) * 3
                    for t2 in range(2):
                        for kc in range(NPAIR):
                            S.op("pe", lambda e, b=b, mc=mc, kc=kc, t2=t2, b0=b0: e.matmul(self.PS[:, b0 + t2, :], lhsT=wd[b][:, kc, mc * 128:(mc + 1) * 128],
                                                                                            rhs=hid[:, kc, t2 * 512:(t2 + 1) * 512], start=(kc == 0), stop=(kc == NPAIR - 1)),
                                 reads=[("B", "wd", b), ("A", "hid", kc)], writes=[("ps", b0 + t2)])
                    S.op("act", lambda e, m=m, b0=b0: e.activation(out=self.hb[:, m, th * TH:(th + 1) * TH], in_=self.bank(b0, 2), func=AF.Identity),
                         reads=[("ps", b0), ("ps", b0 + 1)], writes=[("hb", m, 2 * th), ("hb", m, 2 * th + 1)])
            if th == 0:
                self.dbg("f%d" % l, self.hb[:, :, 0:512], [("hb", kc, 0) for kc in range(KC)], BF16)
            self.postnorm(l, 1, tts=[2 * th, 2 * th + 1])

    def mixer_sconv(self, l):
        S, ar = self.S, self.ar
        S.new_phase("A")
        S.new_phase("B")
        ar.reset()
        gbuf = ar.bf16(KC, T)
        markB = ar.off
        wt = [ar.bf16(8, 384) for _ in range(2)]
        cgs = ar.f32(T)
        pr = ar.f32(T)
        acc = ar.f32(T)
        self.prenorm(l, 0)

        def proj(i, b, colblk, b0):
            for tt in range(4):
                for kc in range(KC):
                    S.op("pe", lambda e, kc=kc, tt=tt: e.matmul(self.PS[:, b0 + tt, :], lhsT=wt[b][:, kc, colblk * 128:(colblk + 1) * 128],
                                                                 rhs=self.hb[:, kc, tt * 512:(tt + 1) * 512], start=(kc == 0), stop=(kc == KC - 1)),
                         reads=[("B", "wt", b), ("hb", kc, tt)], writes=[("ps", b0 + tt)])

        def pk(b0):
            return [("ps", b0 + t) for t in range(4)]

        for i in range(KC):
            b = i % 2
            S.op(self.wq, lambda e, i=i, b=b: e.dma_start(out=wt[b], in_=self.d_scin[i].rearrange("p (k n) -> p k n", k=8)), writes=[("B", "wt", b)], dma=True)
            proj(i, b, 1, 0)
            S.op("act", lambda e: e.activation(out=cgs, in_=self.bank(0, 4), func=AF.Identity), reads=pk(0), writes=[("B", "cgs")])
            proj(i, b, 2, 4)
            S.op("dve", lambda e: e.tensor_tensor(out=pr, in0=cgs, in1=self.bank(4, 4), op=ALU.mult), reads=[("B", "cgs")] + pk(4), writes=[("B", "pr")])
            self.conv3(pr, [("B", "pr")], acc, ("B", "acc"), CW_SC + i, T)
            proj(i, b, 0, 0)
            S.op("dve", lambda e, i=i: e.tensor_tensor(out=gbuf[:, i, :], in0=acc, in1=self.bank(0, 4), op=ALU.mult), reads=[("B", "acc")] + pk(0), writes=[("A", "g", i)])
        S.new_phase("B")
        ar.reset(markB)
        wo = ar.bf16(8, 1024)
        S.op(self.wq, lambda e: e.dma_start(out=wo, in_=self.d_scout.rearrange("p (k n) -> p k n", k=8)), writes=[("B", "wo")], dma=True)
        for mc in range(KC):
            b0 = (mc % 2) * 4
            for tt in range(4):
                for kc in range(KC):
                    S.op("pe", lambda e, kc=kc, tt=tt, mc=mc, b0=b0: e.matmul(self.PS[:, b0 + tt, :], lhsT=wo[:, kc, mc * 128:(mc + 1) * 128],
                                                                              rhs=gbuf[:, kc, tt * 512:(tt + 1) * 512], start=(kc == 0), stop=(kc == KC - 1)),
                         reads=[("B", "wo"), ("A", "g", kc)], writes=[("ps", b0 + tt)])
            S.op("act", lambda e, mc=mc, b0=b0: e.activation(out=self.hb[:, mc, :], in_=self.bank(b0, 4), func=AF.Identity),
                 reads=pk(b0), writes=[("hb", mc, tt) for tt in range(4)])
        self.postnorm(l, 0)

    def build(self):
        self.setup()
        self.phase_mod()
        for l in self.layers:
            if l in self.mixers:
                kind = l % 3
                if kind == 0:
                    self.mixer_ssd(l)
                elif kind == 1:
                    self.mixer_sconv(l)
                else:
                    self.mixer_attn(l)
            self.ffn(l)
        self.finish()

    def finish(self):
        S = self.S
        for kc in range(KC):
            S.op("sp", lambda e, kc=kc: e.dma_start(out=self.o_y[:, kc, :], in_=self.xres[:, kc, :]),
                 reads=[("x", kc, tt) for tt in range(4)], dma=True, is_out=True)
        S.emit()
        self.st.close()


def mixer_ssd(self, l):
    S, ar = self.S, self.ar
    j = l // 3
    AX = mybir.AxisListType.X
    CC = [("c", "consts")]
    CB = [("c", "cbf")]
    S.new_phase("A")
    S.new_phase("B")
    ar.reset()
    dt_all = ar.f32(16, 64)
    dta_all = ar.f32(16, 64)
    hT = ar.f32(2048)
    hTb = ar.bf16(2048)
    mask4 = ar.bf16(4, 128)
    small = ar.f32(8, 32)
    hilo = ar.bf16(2, 32)
    markB = ar.off
    self.prenorm(l, 0)
    wt = [ar.bf16(8, 512) for _ in range(2)]
    acc = ar.f32(T)
    stage = [ar.bf16(T) for _ in range(2)]
    dtT = ar.f32(T)
    dtAT = ar.f32(T)
    wdt = ar.bf16(8, 64)
    z_dst = self.s_z.rearrange("c p n -> p c n")
    x_dst = self.s_xbc.rearrange("c p n -> p c n")
    sti = 0
    for i in range(12):
        b = i % 2
        S.op(self.wq, lambda e, i=i, b=b: e.dma_start(out=wt[b], in_=self.d_sin[j, i].rearrange("p (k n) -> p k n", k=8)), writes=[("B", "wt", b)], dma=True)
        for c in range(4):
            f = 4 * i + c
            b0 = (f % 2) * 4
            for tt in range(4):
                for kc in range(KC):
                    S.op("pe", lambda e, b=b, c=c, kc=kc, tt=tt, b0=b0: e.matmul(self.PS[:, b0 + tt, :], lhsT=wt[b][:, kc, c * 128:(c + 1) * 128],
                                                                                  rhs=self.hb[:, kc, tt * 512:(tt + 1) * 512], start=(kc == 0), stop=(kc == KC - 1)),
                         reads=[("B", "wt", b), ("hb", kc, tt)], writes=[("ps", b0 + tt)])
            pk = [("ps", b0 + t) for t in range(4)]
            si = sti % 2
            sti += 1
            st_ = stage[si]
            if f < 16:
                S.op("act", lambda e, b0=b0, st_=st_: e.activation(out=st_, in_=self.bank(b0, 4), func=AF.Silu), reads=pk, writes=[("B", "stage", si)])
                S.op("sp", lambda e, f=f, st_=st_: e.dma_start(out=z_dst[:, :, f * 128:(f + 1) * 128], in_=st_.rearrange("p (c t) -> p c t", c=16)),
                     reads=[("B", "stage", si)], writes=[("dr", "z", f)], dma=True)
            else:
                fx = f - 16
                self.conv3(self.bank(b0, 4), pk, acc, ("B", "acc"), CW_SSD + j * 32 + fx, T)
                cbo = CMAP["ssd_cb"][0] + j * 32 + fx
                S.op("act", lambda e, st_=st_, cbo=cbo: e.activation(out=st_, in_=acc, func=AF.Silu, bias=self.consts[:, cbo:cbo + 1], scale=1.0),
                     reads=[("B", "acc")] + CC, writes=[("B", "stage", si)])
                S.op("sp", lambda e, fx=fx, st_=st_: e.dma_start(out=x_dst[:, :, fx * 128:(fx + 1) * 128], in_=st_.rearrange("p (c t) -> p c t", c=16)),
                     reads=[("B", "stage", si)], writes=[("dr", "xbc", fx)], dma=True)
    S.op(self.wq, lambda e: e.dma_start(out=wdt, in_=self.d_sdt[j].rearrange("p (k n) -> p k n", k=8)), writes=[("B", "wdt")], dma=True)
    for tt in range(4):
        for kc in range(KC):
            S.op("pe", lambda e, kc=kc, tt=tt: e.matmul(self.PS[0:64, tt, :], lhsT=wdt[:, kc, :], rhs=self.hb[:, kc, tt * 512:(tt + 1) * 512], start=(kc == 0), stop=(kc == KC - 1)),
                 reads=[("B", "wdt"), ("hb", kc, tt)], writes=[("ps", tt)])
    pk = [("ps", t) for t in range(4)]
    dbo = CMAP["ssd_dtb"][0] + j
    alo = CMAP["ssd_alog"][0] + j
    ps64 = self.PS[0:64, 0:4, :].rearrange("p b n -> p (b n)")
    S.op("act", lambda e: e.activation(out=dtT[0:64, :], in_=ps64, func=AF.Exp, bias=self.consts[0:64, dbo:dbo + 1], scale=1.0), reads=pk + CC, writes=[("B", "dtT")])
    S.op("act", lambda e: e.activation(out=dtT[0:64, :], in_=dtT[0:64, :], func=AF.Ln, bias=1.0, scale=1.0), reads=[("B", "dtT")], writes=[("B", "dtT")])
    S.op("act", lambda e: e.activation(out=self.misc[0:64, 8:9], in_=self.consts[0:64, alo:alo + 1], func=AF.Exp), reads=CC, writes=[("c", "misc")])
    S.op("dve", lambda e: e.tensor_scalar(out=self.misc[0:64, 8:9], in0=self.misc[0:64, 8:9], scalar1=-1.0, scalar2=None, op0=ALU.mult), reads=[("c", "misc")], writes=[("c", "misc")])
    S.op("dve", lambda e: e.tensor_scalar(out=dtAT[0:64, :], in0=dtT[0:64, :], scalar1=self.misc[0:64, 8:9], scalar2=None, op0=ALU.mult),
         reads=[("B", "dtT"), ("c", "misc")], writes=[("B", "dtAT")])
    id32 = self.c("ident")
    for which, (src_, dst_, key) in enumerate(((dtT, dt_all, ("B", "dtT")), (dtAT, dta_all, ("B", "dtAT")))):
        for tc in range(16):
            bk = 4 + which * 2 + tc // 8
            off = (tc % 8) * 64
            S.op("pe", lambda e, src_=src_, tc=tc, bk=bk, off=off: e.matmul(self.PS[:, bk, off:off + 64], lhsT=src_[0:64, tc * 128:(tc + 1) * 128], rhs=id32[0:64, 0:64], start=True, stop=True),
                 reads=[key] + CC, writes=[("ps", bk)])
        for hb_ in range(2):
            bk = 4 + which * 2 + hb_
            S.op("act", lambda e, dst_=dst_, bk=bk, hb_=hb_: e.activation(out=dst_[:, hb_ * 8:(hb_ + 1) * 8, :], in_=self.PS[:, bk, :].rearrange("p (a b) -> p a b", a=8), func=AF.Identity),
                 reads=[("ps", bk)], writes=[("A", "dtall")])
    for d in range(2):
        S.new_phase("B")
        ar.reset(markB)
        xb = [ar.bf16(32, 128) for _ in range(2)]
        b_tok = ar.bf16(1024)
        xdt = ar.bf16(2048)
        xw = ar.bf16(2048)
        Rhl = [[ar.bf16(4, 128) for _ in range(2)] for _ in range(3)]
        Et = [ar.bf16(4, 128) for _ in range(3)]
        Mt = [ar.bf16(4, 128) for _ in range(3)]
        cbs = [ar.bf16(128) for _ in range(3)]
        if d == 1:
            zT = ar.bf16(16, 128)
            yf = ar.bf16(2048)
            gat = ar.bf16(16, 128)
            tmp8 = ar.f32(8, 128)
            sqb = self.sq[:, :, :].rearrange("p a (b t) -> p (a b) t", t=128)
            rs = ar.f32(128)
            ynT = gat
        else:
            ychb = ar.bf16(2048)
        tri32 = self.c("triU" if d == 0 else "triL")
        one32 = self.c("ones")
        triq = self.cbf[:, 2 + d, :]
        Lm = self.cbf[:, 4 + d, :]
        mk = self.cbf[:, 6 + d, :]
        S.op("dve", lambda e, mk=mk: e.tensor_copy(out=mask4, in_=mk.unsqueeze(1).broadcast_to([128, 4, 128])), reads=CB, writes=[("A", "mask4")])
        S.op("sp", lambda e, d=d: e.dma_start(out=hT, in_=self.d_ssm0[j, d]), writes=[("A", "hT")], dma=True)
        hs = slice(d * 32, (d + 1) * 32)
        order = list(range(16)) if d == 0 else list(range(15, -1, -1))

        def load_x(ci):
            S.op("sp", lambda e: e.dma_start(out=xb[ci % 2], in_=self.s_xbc[order[ci]].rearrange("p (c t) -> p c t", c=32)),
                 reads=[("dr", "xbc", f_) for f_ in range(32)], writes=[("B", "xbcT", ci % 2)], dma=True)

        load_x(0)
        pending_reset = False
        for ci, tc in enumerate(order):
            if ci + 1 < 16:
                load_x(ci + 1)
            xbcT = xb[ci % 2]
            XK = ("B", "xbcT", ci % 2)
            if d == 1:
                S.op("sp", lambda e, tc=tc: e.dma_start(out=zT, in_=self.s_z[tc].rearrange("p (c t) -> p c t", c=16)), reads=[("dr", "z", f_) for f_ in range(16)], writes=[("B", "zT")], dma=True)
                S.op("sp", lambda e, tc=tc: e.dma_start(out=yf, in_=self.s_yf[tc]), reads=[("dr", "yf", tc)], writes=[("B", "yf", g_) for g_ in range(8)] + [("B", "yo", g_) for g_ in range(8)], dma=True)
            if pending_reset:
                S.op("act", lambda e: e.activation(out=hTb, in_=hT, func=AF.Identity, scale=self.c("flag")), reads=[("A", "hT")] + CC, writes=[("A", "hTb")])
            else:
                S.op("act", lambda e: e.activation(out=hTb, in_=hT, func=AF.Identity), reads=[("A", "hT")], writes=[("A", "hTb")])
            for fc in range(24):
                bk = fc // 8
                psb = self.PS[:, bk, :].bitcast(BF16)
                S.op("pe", lambda e, fc=fc, psb=psb: e.transpose(out=psb[:, (fc % 8) * 128:(fc % 8 + 1) * 128], in_=xbcT[:, fc, :], identity=self.ident()),
                     reads=[XK] + CB, writes=[("ps", bk)])
            S.op("act", lambda e: e.activation(out=b_tok, in_=self.PS[:, 2, :].bitcast(BF16), func=AF.Identity), reads=[("ps", 2)], writes=[("B", "btok")])
            dta = dta_all[:, tc, hs]
            dtk = dt_all[:, tc, hs]
            S.op("pe", lambda e, dta=dta: e.matmul(self.PS[:, 3, 0:32], lhsT=tri32, rhs=dta, start=True, stop=True), reads=[("A", "dtall")] + CC, writes=[("ps", 3)])
            S.op("pe", lambda e, dta=dta: e.matmul(self.PS[:, 3, 32:64], lhsT=one32, rhs=dta, start=True, stop=True), reads=[("A", "dtall")] + CC, writes=[("ps", 3)])
            SK = [("B", "small")]
            S.op("act", lambda e: e.activation(out=small[:, 0, :], in_=self.PS[:, 3, 0:32], func=AF.Identity), reads=[("ps", 3)], writes=SK)
            S.op("act", lambda e: e.activation(out=small[:, 1, :], in_=self.PS[:, 3, 0:32], func=AF.Exp), reads=[("ps", 3)], writes=SK)
            S.op("act", lambda e: e.activation(out=small[:, 3, :], in_=self.PS[:, 3, 32:64], func=AF.Exp), reads=[("ps", 3)], writes=SK)
            S.op("dve", lambda e: e.tensor_tensor(out=small[:, 4, :], in0=self.PS[:, 3, 32:64], in1=small[:, 0, :], op=ALU.subtract), reads=[("ps", 3)] + SK, writes=SK)
            S.op("act", lambda e: e.activation(out=small[:, 2, :], in_=small[:, 4, :], func=AF.Exp), reads=SK, writes=SK)
            S.op("dve", lambda e, dta=dta: e.tensor_copy(out=hilo[:, 0, :], in_=dta), reads=[("A", "dtall")], writes=[("B", "hilo")])
            S.op("dve", lambda e, dta=dta: e.tensor_tensor(out=hilo[:, 1, :], in0=dta, in1=hilo[:, 0, :], op=ALU.subtract), reads=[("A", "dtall"), ("B", "hilo")], writes=[("B", "hilo")])
            for bk in range(2):
                S.op("dve", lambda e, bk=bk, dtk=dtk: e.tensor_tensor(out=xdt[:, bk * 1024:(bk + 1) * 1024].rearrange("p (h q) -> p h q", h=16),
                                                                       in0=self.PS[:, bk, :].bitcast(BF16).rearrange("p (h q) -> p h q", h=16),
                                                                       in1=dtk[:, bk * 16:(bk + 1) * 16].unsqueeze(2).broadcast_to([128, 16, 64]), op=ALU.mult),
                     reads=[("ps", bk), ("A", "dtall")], writes=[("B", "xdt")])
            S.op("dve", lambda e: e.tensor_tensor(out=xw.rearrange("p (h q) -> p h q", h=32), in0=xdt.rearrange("p (h q) -> p h q", h=32), in1=small[:, 2, :].unsqueeze(2).broadcast_to([128, 32, 64]), op=ALU.mult),
                 reads=[("B", "xdt")] + SK, writes=[("B", "xw")])
            E2 = lambda a: a.rearrange("p a b -> p (a b)")
            ydst = ychb if d == 0 else yf

            def front(g):
                gb = g % 3
                R = Rhl[gb]
                dbk = (4, 6, 0)[gb]
                cb = self.PS[:, 3, 128 + gb * 128: 256 + gb * 128]
                for hl in range(2):
                    S.op("dve", lambda e, hl=hl: e.tensor_tensor(out=R[hl], in0=triq.unsqueeze(1).broadcast_to([128, 4, 128]),
                                                                  in1=hilo[:, hl, 4 * g:4 * g + 4].unsqueeze(2).broadcast_to([128, 4, 128]), op=ALU.mult),
                         reads=CB + [("B", "hilo")], writes=[("B", "R", gb, hl)])
                S.op("pe", lambda e: e.matmul(self.PS[:, dbk, :], lhsT=Lm, rhs=E2(R[0]), start=True, stop=False), reads=CB + [("B", "R", gb, 0)], writes=[("ps", dbk)])
                S.op("pe", lambda e: e.matmul(self.PS[:, dbk, :], lhsT=Lm, rhs=E2(R[1]), start=False, stop=False), reads=CB + [("B", "R", gb, 1)], writes=[("ps", dbk)])
                S.op("pe", lambda e: e.matmul(self.PS[:, dbk, :], lhsT=self.ident(), rhs=E2(mask4), start=False, stop=True), reads=CB + [("A", "mask4")], writes=[("ps", dbk)])
                S.op("act", lambda e: e.activation(out=E2(Et[gb]), in_=self.PS[:, dbk, :], func=AF.Exp), reads=[("ps", dbk)], writes=[("B", "Et", gb)])
                S.op("pe", lambda e: e.matmul(cb, lhsT=xbcT[:, 16 + g, :], rhs=xbcT[:, 24 + g, :], start=True, stop=True), reads=[XK], writes=[("ps", 3)])
                S.op("act", lambda e: e.activation(out=cbs[gb], in_=cb, func=AF.Identity), reads=[("ps", 3)], writes=[("B", "cbs", gb)])

            def back(g):
                gb = g % 3
                cb = self.PS[:, 3, 128 + gb * 128: 256 + gb * 128]
                S.op("dve", lambda e: e.tensor_tensor(out=Mt[gb], in0=Et[gb], in1=cbs[gb].unsqueeze(1).broadcast_to([128, 4, 128]), op=ALU.mult),
                     reads=[("B", "Et", gb), ("B", "cbs", gb)], writes=[("B", "Mt", gb)])
                ybk = (5, 7, 1)[gb]
                tcb = (self.tmpf[:, 0, 0:256], self.tmpf[:, 1, 0:256], self.tmpf[:, 0, 256:512])[gb]
                tkey = ("c", "tmpf", gb % 2)
                if d == 1:
                    S.op("pe", lambda e: e.matmul(self.PS[:, ybk, 0:256], lhsT=self.ident(), rhs=yf[:, g * 256:(g + 1) * 256], start=True, stop=False),
                         reads=CB + [("B", "yf", g)], writes=[("ps", ybk)])
                for h4 in range(4):
                    h = 4 * g + h4
                    S.op("pe", lambda e, h4=h4, h=h: e.matmul(self.PS[:, ybk, h4 * 64:(h4 + 1) * 64], lhsT=Mt[gb][:, h4, :], rhs=xdt[:, h * 64:(h + 1) * 64],
                                                               start=(d == 0), stop=(d == 0 or h4 == 3)),
                         reads=[("B", "Mt", gb), ("B", "xdt")], writes=[("ps", ybk)])
                S.op("pe", lambda e: e.matmul(self.PS[:, ybk, 256:512], lhsT=xbcT[:, 24 + g, :], rhs=hTb[:, g * 256:(g + 1) * 256], start=True, stop=True),
                     reads=[XK, ("A", "hTb")], writes=[("ps", ybk)])
                S.op("dve", lambda e: e.tensor_tensor(out=tcb.rearrange("p (h q) -> p h q", h=4), in0=self.PS[:, ybk, 256:512].rearrange("p (h q) -> p h q", h=4),
                                                       in1=small[:, 1, 4 * g:4 * g + 4].unsqueeze(2).broadcast_to([128, 4, 64]), op=ALU.mult),
                     reads=[("ps", ybk)] + SK, writes=[tkey])
                S.op("dve", lambda e: e.tensor_tensor(out=ydst[:, g * 256:(g + 1) * 256], in0=tcb, in1=self.PS[:, ybk, 0:256], op=ALU.add),
                     reads=[("ps", ybk), tkey], writes=[("B", "yo", g)] + ([("B", "yf", g)] if d == 1 else []))

            front(0)
            front(1)
            for g in range(8):
                if g + 2 < 8:
                    front(g + 2)
                back(g)
            for g in range(8):
                bk, off = g // 2, (g % 2) * 256
                S.op("pe", lambda e, g=g, bk=bk, off=off: e.matmul(self.PS[:, bk, off:off + 256], lhsT=b_tok[:, g * 128:(g + 1) * 128], rhs=xw[:, g * 256:(g + 1) * 256], start=True, stop=True),
                     reads=[("B", "btok"), ("B", "xw")], writes=[("ps", bk)])
            h3 = hT.rearrange("p (h q) -> p h q", h=32)
            if pending_reset:
                S.op("dve", lambda e: e.tensor_scalar(out=small[:, 3, :], in0=small[:, 3, :], scalar1=self.c("flag"), scalar2=None, op0=ALU.mult), reads=SK + CC, writes=SK)
                pending_reset = False
            S.op("dve", lambda e, h3=h3: e.tensor_tensor(out=h3, in0=h3, in1=small[:, 3, :].unsqueeze(2).broadcast_to([128, 32, 64]), op=ALU.mult), reads=[("A", "hT")] + SK, writes=[("A", "hT")])
            S.op("dve", lambda e: e.tensor_tensor(out=hT, in0=hT, in1=self.bank(0, 4), op=ALU.add), reads=[("A", "hT")] + [("ps", t) for t in range(4)], writes=[("A", "hT")])
            seg_end = (tc % 2 == 1) if d == 0 else (tc % 2 == 0)
            if seg_end:
                seg = tc // 2
                S.op("sp", lambda e, seg=seg, d=d: e.dma_start(out=self.o_st[j, seg, d], in_=hT), reads=[("A", "hT")], dma=True, is_out=True)
                pending_reset = True
            YO = [("B", "yo", g_) for g_ in range(8)]
            if d == 0:
                S.op("sp", lambda e, tc=tc: e.dma_start(out=self.s_yf[tc], in_=ychb), reads=YO, writes=[("dr", "yf", tc)], dma=True)
            else:
                for fc in range(16):
                    bk = 6 + fc // 8
                    psb = self.PS[:, bk, :].bitcast(BF16)
                    S.op("pe", lambda e, fc=fc, psb=psb: e.transpose(out=psb[:, (fc % 8) * 128:(fc % 8 + 1) * 128], in_=yf[:, fc * 128:(fc + 1) * 128], identity=self.ident()),
                         reads=[("B", "yo", fc // 2)] + CB, writes=[("ps", bk)])
                do = CMAP["ssd_d"][0] + j * 16
                go = CMAP["ssd_ng"][0] + j * 16
                for hf in range(2):
                    fs = slice(hf * 8, (hf + 1) * 8)
                    psb = self.PS[:, 6 + hf, :].bitcast(BF16).rearrange("p (a b) -> p a b", a=8)
                    S.op("dve", lambda e, fs=fs, hf=hf: e.tensor_tensor(out=tmp8, in0=xbcT[:, fs, :], in1=self.consts[:, do + hf * 8: do + hf * 8 + 8].unsqueeze(2).broadcast_to([128, 8, 128]), op=ALU.mult),
                         reads=[XK] + CC, writes=[("B", "tmp8")])
                    S.op("dve", lambda e, psb=psb: e.tensor_tensor(out=tmp8, in0=tmp8, in1=psb, op=ALU.add), reads=[("B", "tmp8"), ("ps", 6 + hf)], writes=[("B", "tmp8")])
                    S.op("dve", lambda e, fs=fs: e.tensor_tensor(out=gat[:, fs, :], in0=tmp8, in1=zT[:, fs, :], op=ALU.mult), reads=[("B", "tmp8"), ("B", "zT")], writes=[("B", "gat", hf)])
                    S.op("act", lambda e, fs=fs: e.activation(out=sqb, in_=gat[:, fs, :], func=AF.Square), reads=[("B", "gat", hf)], writes=[("c", "sq", 0), ("c", "sq", 1)])
                    for f8 in range(8):
                        fc = hf * 8 + f8
                        S.op("pe", lambda e, fc=fc, f8=f8: e.matmul(self.PS[:, 4, 0:128], lhsT=self.ones(), rhs=sqb[:, f8, :], start=(fc == 0), stop=(fc == 15)), reads=[("c", "sq", 0), ("c", "sq", 1)] + CB, writes=[("ps", 4)])
                S.op("act", lambda e: e.activation(out=rs, in_=self.PS[:, 4, 0:128], func=AF.Sqrt, scale=1.0 / 2048, bias=EPS), reads=[("ps", 4)], writes=[("B", "rs")])
                S.op("dve", lambda e: e.reciprocal(out=rs, in_=rs), reads=[("B", "rs")], writes=[("B", "rs")])
                for hf in range(2):
                    fs = slice(hf * 8, (hf + 1) * 8)
                    S.op("dve", lambda e, fs=fs: e.tensor_tensor(out=tmp8, in0=gat[:, fs, :], in1=rs.unsqueeze(1).broadcast_to([128, 8, 128]), op=ALU.mult), reads=[("B", "gat", hf), ("B", "rs")], writes=[("B", "tmp8")])
                    S.op("dve", lambda e, fs=fs, hf=hf: e.tensor_tensor(out=gat[:, fs, :], in0=tmp8, in1=self.consts[:, go + hf * 8: go + hf * 8 + 8].unsqueeze(2).broadcast_to([128, 8, 128]), op=ALU.mult),
                         reads=[("B", "tmp8")] + CC, writes=[("B", "gat", hf)])
                S.op("sp", lambda e, tc=tc: e.dma_start(out=self.s_yn.rearrange("f p t -> p f t")[:, :, tc * 128:(tc + 1) * 128], in_=ynT), reads=[("B", "gat", 0), ("B", "gat", 1)], writes=[("dr", "yn", tc)], dma=True)
    TH = 1024
    for th in range(2):
        S.new_phase("A")
        S.new_phase("B")
        ar.reset()
        yh = ar.bf16(16, TH)
        wo = [ar.bf16(16, 256) for _ in range(2)]
        S.op("sp", lambda e, th=th: e.dma_start(out=yh, in_=self.s_yn.rearrange("f p t -> p f t")[:, :, th * TH:(th + 1) * TH]), reads=[("dr", "yn", t_) for t_ in range(16)], writes=[("A", "yh")], dma=True)
        for mt in range(4):
            b = mt % 2
            S.op(self.wq, lambda e, mt=mt, b=b: e.dma_start(out=wo[b], in_=self.d_sout[j, mt].rearrange("p (k n) -> p k n", k=16)), writes=[("B", "wo", b)], dma=True)
            for mc in range(2):
                m = mt * 2 + mc
                b0 = (m % 2) * 2
                for t2 in range(2):
                    for kc in range(16):
                        S.op("pe", lambda e, b=b, mc=mc, kc=kc, t2=t2, b0=b0: e.matmul(self.PS[:, b0 + t2, :], lhsT=wo[b][:, kc, mc * 128:(mc + 1) * 128], rhs=yh[:, kc, t2 * 512:(t2 + 1) * 512],
                                                                                        start=(kc == 0), stop=(kc == 15)),
                             reads=[("B", "wo", b), ("A", "yh")], writes=[("ps", b0 + t2)])
                S.op("act", lambda e, m=m, b0=b0, th=th: e.activation(out=self.hb[:, m, th * TH:(th + 1) * TH], in_=self.bank(b0, 2), func=AF.Identity),
                     reads=[("ps", b0), ("ps", b0 + 1)], writes=[("hb", m, 2 * th), ("hb", m, 2 * th + 1)])
    self.postnorm(l, 0)


Builder.mixer_ssd = mixer_ssd


def mixer_attn(self, l):
    import math
    S, ar = self.S, self.ar
    lam_init = 0.8 - 0.6 * math.exp(-0.3 * l)
    S.new_phase("A")
    S.new_phase("B")
    ar.reset()
    obuf = ar.bf16(KC, T)
    sinT = ar.f32(T)
    cosT = self.rstd
    markB = ar.off
    wt = ar.bf16(8, 384)
    qT = ar.bf16(T)
    kT2 = [ar.bf16(T + 256) for _ in range(2)]
    vtok = ar.bf16(18, 128)
    stg = [ar.f32(512) for _ in range(3)]
    vst = [ar.f32(512) for _ in range(2)]
    et = [self.sq[:, 0, :], self.sq[:, 1, :], ar.bf16(512), ar.bf16(512), ar.bf16(512)]
    etk = [("c", "sq", 0), ("c", "sq", 1), ("B", "et", 2), ("B", "et", 3), ("B", "et", 4)]
    sqt = ar.bf16(512)
    misc = self.misc
    CC = [("c", "consts")]
    self.prenorm(l, 0)
    S.op("dve", lambda e: e.memset(kT2[0][64:128, :], 0.0), writes=[("B", "kT", 0)])
    S.op("dve", lambda e: e.memset(kT2[1][0:64, :], 0.0), writes=[("B", "kT", 1)])
    S.op("sp", lambda e: e.dma_start(out=cosT[:, :], in_=self.d_rot[:, 0, :]), writes=[("c", "rstd", tt) for tt in range(4)], dma=True)
    S.op("sp", lambda e: e.dma_start(out=sinT, in_=self.d_rot[:, 1, :]), writes=[("A", "sin")], dma=True)
    lo = CMAP["lam"][0]
    lp = lambda i: self.consts[:, lo + i * 64: lo + (i + 1) * 64]
    for i in range(2):
        S.op("dve", lambda e, i=i: e.tensor_tensor(out=stg[0][:, i * 64:(i + 1) * 64], in0=lp(2 * i), in1=lp(2 * i + 1), op=ALU.mult), reads=CC, writes=[("B", "stg", 0)])
        S.op("dve", lambda e, i=i: e.reduce_sum(out=misc[:, i:i + 1], in_=stg[0][:, i * 64:(i + 1) * 64], axis=mybir.AxisListType.X), reads=[("B", "stg", 0)], writes=[("c", "misc")])
    S.op("act", lambda e: e.activation(out=misc[:, 2:4], in_=misc[:, 0:2], func=AF.Exp), reads=[("c", "misc")], writes=[("c", "misc")])
    S.op("dve", lambda e: e.tensor_tensor(out=misc[:, 4:5], in0=misc[:, 3:4], in1=misc[:, 2:3], op=ALU.subtract), reads=[("c", "misc")], writes=[("c", "misc")])
    S.op("dve", lambda e: e.tensor_scalar(out=misc[:, 4:5], in0=misc[:, 4:5], scalar1=-lam_init, scalar2=None, op0=ALU.add), reads=[("c", "misc")], writes=[("c", "misc")])
    S.op("dve", lambda e: e.tensor_scalar(out=misc[:, 5:6], in0=self.c("subg"), scalar1=1.0 - lam_init, scalar2=None, op0=ALU.mult), reads=CC + [("c", "misc")], writes=[("c", "misc")])
    neglam = misc[:, 4:5]
    gsub = misc[:, 5:6]
    ao = CMAP["abias"][0]
    rotp = self.c("rotp")
    si = [0]
    ei = [0]

    def proj_fm(colblk):
        for tt in range(4):
            for kc in range(KC):
                S.op("pe", lambda e, kc=kc, tt=tt: e.matmul(self.PS[:, tt, :], lhsT=wt[:, kc, colblk * 128:(colblk + 1) * 128],
                                                             rhs=self.hb[:, kc, tt * 512:(tt + 1) * 512], start=(kc == 0), stop=(kc == KC - 1)),
                     reads=[("B", "wt"), ("hb", kc, tt)], writes=[("ps", tt)])

    def rotary(dst, dst_off, hp, is_k):
        for tt in range(4):
            ts = slice(tt * 512, (tt + 1) * 512)
            i = si[0] % 3
            si[0] += 1
            j = si[0] % 3
            si[0] += 1
            pb = 4 + (tt % 2)
            S.op("act", lambda e, tt=tt, i=i: e.activation(out=stg[i], in_=self.PS[:, tt, :], func=AF.Identity), reads=[("ps", tt)], writes=[("B", "stg", i)])
            if is_k:
                S.op("sp", lambda e, tt=tt, i=i, ts=ts: e.dma_start(out=self.o_k[:, hp, ts], in_=stg[i]), reads=[("B", "stg", i)], dma=True, is_out=True)
            S.op("pe", lambda e, i=i, pb=pb: e.matmul(self.PS[:, pb, :], lhsT=rotp, rhs=stg[i], start=True, stop=True),
                 reads=[("B", "stg", i)] + CC, writes=[("ps", pb)])
            S.op("dve", lambda e, j=j, pb=pb, ts=ts: e.tensor_tensor(out=stg[j], in0=self.PS[:, pb, :], in1=sinT[:, ts], op=ALU.mult),
                 reads=[("ps", pb), ("A", "sin")], writes=[("B", "stg", j)])
            S.op("dve", lambda e, i=i, ts=ts, tt=tt: e.tensor_tensor(out=stg[i], in0=stg[i], in1=cosT[:, ts], op=ALU.mult),
                 reads=[("B", "stg", i), ("c", "rstd", tt)], writes=[("B", "stg", i)])
            for (dap, rows, dkey) in dst:
                S.op("dve", lambda e, i=i, j=j, tt=tt, dap=dap, rows=rows: e.tensor_tensor(out=dap[rows, dst_off + tt * 512: dst_off + (tt + 1) * 512], in0=stg[i][rows, :], in1=stg[j][rows, :], op=ALU.add),
                     reads=[("B", "stg", i), ("B", "stg", j)], writes=[dkey])

    for hp in range(8):
        S.op(self.wq, lambda e, hp=hp: e.dma_start(out=wt, in_=self.d_qkv[hp].rearrange("p (k n) -> p k n", k=8)), writes=[("B", "wt")], dma=True)
        S.op(self.wq, lambda e, hp=hp: e.dma_start(out=kT2[0][0:64, 0:256], in_=self.d_kc[0:64, hp, :]), writes=[("B", "kT", 0)], dma=True)
        S.op(self.wq, lambda e, hp=hp: e.dma_start(out=kT2[1][64:128, 0:256], in_=self.d_kc[64:128, hp, :]), writes=[("B", "kT", 1)], dma=True)
        S.op(self.wq, lambda e, hp=hp: e.dma_start(out=vtok[:, 0:2, :], in_=self.d_vc[:, :, hp * 128:(hp + 1) * 128]), writes=[("B", "vtok")], dma=True)
        proj_fm(0)
        rotary([(qT, slice(0, 128), ("B", "qT"))], 0, hp, False)
        proj_fm(1)
        rotary([(kT2[0], slice(0, 64), ("B", "kT", 0)), (kT2[1], slice(64, 128), ("B", "kT", 1))], 256, hp, True)
        for blk in range(16):
            bk, off = blk // 4, (blk % 4) * 128
            for kc in range(KC):
                S.op("pe", lambda e, kc=kc, blk=blk, bk=bk, off=off: e.matmul(self.PS[:, bk, off:off + 128], lhsT=self.hb[:, kc, blk * 128:(blk + 1) * 128],
                                                                              rhs=wt[:, kc, 256:384], start=(kc == 0), stop=(kc == KC - 1)),
                     reads=[("B", "wt"), ("hb", kc, blk // 4)], writes=[("ps", bk)])
        for bk in range(4):
            i = bk % 2
            S.op("act", lambda e, bk=bk, i=i: e.activation(out=vst[i], in_=self.PS[:, bk, :], func=AF.Identity), reads=[("ps", bk)], writes=[("B", "vst", i)])
            S.op("sp", lambda e, bk=bk, i=i, hp=hp: e.dma_start(out=self.o_v[:, 4 * bk:4 * bk + 4, hp * 128:(hp + 1) * 128], in_=vst[i].rearrange("p (a b) -> p a b", a=4)),
                 reads=[("B", "vst", i)], dma=True, is_out=True)
            S.op("dve", lambda e, bk=bk, i=i: e.tensor_copy(out=vtok[:, 2 + 4 * bk: 6 + 4 * bk, :], in_=vst[i].rearrange("p (a b) -> p a b", a=4)),
                 reads=[("B", "vst", i)], writes=[("B", "vtok")])
        items = [(qt, kb, hh) for qt in range(4) for kb in range(18) for hh in range(2)]
        LA = 3
        bA, bB, bC = stg
        kA, kB, kC = [("B", "stg", i) for i in range(3)]

        def emit_s(i, hp=hp):
            qt, kb, hh = items[i]
            qs = slice(qt * 512, (qt + 1) * 512)
            rows = slice(hh * 64, (hh + 1) * 64)
            sb, eb = i % 4, i % 5
            S.op("pe", lambda e: e.matmul(self.PS[:, sb, :], lhsT=kT2[hh][:, kb * 128:(kb + 1) * 128], rhs=qT[:, qs], start=True, stop=True),
                 reads=[("B", "kT", hh), ("B", "qT")], writes=[("ps", sb)])
            col = ao + kb * 8 + qt * 2
            ps3 = self.PS[:, sb, :].rearrange("p (a b) -> p a b", a=2)
            S.op("dve", lambda e: e.tensor_tensor(out=ps3, in0=ps3, in1=self.consts[:, col:col + 2].unsqueeze(2).broadcast_to([128, 2, 256]), op=ALU.add),
                 reads=[("ps", sb)] + CC, writes=[("ps", sb)])
            S.op("act", lambda e: e.activation(out=et[eb], in_=self.PS[:, sb, :], func=AF.Exp, scale=0.125), reads=[("ps", sb)], writes=[etk[eb]])

        def emit_pv(i, hp=hp):
            qt, kb, hh = items[i]
            qs = slice(qt * 512, (qt + 1) * 512)
            eb = i % 5
            S.op("pe", lambda e: e.matmul(self.PS[:, 4 + hh, :], lhsT=vtok[:, kb, :], rhs=et[eb], start=(kb == 0), stop=(kb == 17)),
                 reads=[("B", "vtok"), etk[eb]], writes=[("ps", 4 + hh)])
            S.op("pe", lambda e: e.matmul(self.PS[:, 6 + hh, :], lhsT=self.ones(), rhs=et[eb], start=(kb == 0), stop=(kb == 17)),
                 reads=[("c", "cbf"), etk[eb]], writes=[("ps", 6 + hh)])
            if not (kb == 17 and hh == 1):
                return
            S.op("dve", lambda e: e.tensor_copy(out=bA, in_=self.PS[:, 6, :]), reads=[("ps", 6)], writes=[kA])
            S.op("dve", lambda e: e.tensor_copy(out=bB, in_=self.PS[:, 7, :]), reads=[("ps", 7)], writes=[kB])
            S.op("dve", lambda e: e.reciprocal(out=bA, in_=bA), reads=[kA], writes=[kA])
            S.op("dve", lambda e: e.reciprocal(out=bB, in_=bB), reads=[kB], writes=[kB])
            S.op("dve", lambda e: e.tensor_tensor(out=bA, in0=self.PS[:, 4, :], in1=bA, op=ALU.mult), reads=[("ps", 4), kA], writes=[kA])
            S.op("dve", lambda e: e.tensor_tensor(out=bB, in0=self.PS[:, 5, :], in1=bB, op=ALU.mult), reads=[("ps", 5), kB], writes=[kB])
            S.op("dve", lambda e: e.scalar_tensor_tensor(out=bA, in0=bB, scalar=neglam, in1=bA, op0=ALU.mult, op1=ALU.add), reads=[kA, kB, ("c", "misc")], writes=[kA])
            S.op("dve", lambda e: e.tensor_tensor(out=sqt, in0=bA, in1=bA, op=ALU.mult), reads=[kA], writes=[("B", "sqt")])
            S.op("pe", lambda e: e.matmul(self.PS[:, 6, :], lhsT=self.ones(), rhs=sqt, start=True, stop=True), reads=[("c", "cbf"), ("B", "sqt")], writes=[("ps", 6)])
            S.op("act", lambda e: e.activation(out=bC, in_=self.PS[:, 6, :], func=AF.Ln, scale=1.0 / 128, bias=EPS), reads=[("ps", 6)], writes=[kC])
            S.op("act", lambda e: e.activation(out=bC, in_=bC, func=AF.Exp, scale=-0.5), reads=[kC], writes=[kC])
            S.op("dve", lambda e: e.scalar_tensor_tensor(out=obuf[:, hp, qs], in0=bA, scalar=gsub, in1=bC, op0=ALU.mult, op1=ALU.mult),
                 reads=[kA, kC, ("c", "misc")], writes=[("A", "o", hp)])

        for i in range(len(items) + LA):
            if i < len(items):
                emit_s(i)
            if i >= LA:
                emit_pv(i - LA)
    S.new_phase("B")
    ar.reset(markB)
    wo = ar.bf16(8, 1024)
    S.op(self.wq, lambda e: e.dma_start(out=wo, in_=self.d_dout.rearrange("p (k n) -> p k n", k=8)), writes=[("B", "wo")], dma=True)
    for mc in range(KC):
        b0 = (mc % 2) * 4
        for tt in range(4):
            for kc in range(KC):
                S.op("pe", lambda e, kc=kc, tt=tt, mc=mc, b0=b0: e.matmul(self.PS[:, b0 + tt, :], lhsT=wo[:, kc, mc * 128:(mc + 1) * 128],
                                                                          rhs=obuf[:, kc, tt * 512:(tt + 1) * 512], start=(kc == 0), stop=(kc == KC - 1)),
                     reads=[("B", "wo"), ("A", "o", kc)], writes=[("ps", b0 + tt)])
        S.op("act", lambda e, mc=mc, b0=b0: e.activation(out=self.hb[:, mc, :], in_=self.bank(b0, 4), func=AF.Identity),
             reads=[("ps", b0 + t) for t in range(4)], writes=[("hb", mc, tt) for tt in range(4)])
    self.postnorm(l, 0)


Builder.mixer_attn = mixer_attn


def _fm(v, nch):
    return np.ascontiguousarray(np.asarray(v, np.float32).reshape(nch, 128).T)


def _tile_cols(w, col_lists):
    K = w.shape[0]
    kc = K // 128
    out = []
    for cols in col_lists:
        t = w[:, cols].reshape(kc, 128, len(cols)).transpose(1, 0, 2)
        out.append(t.reshape(128, kc * len(cols)))
    return np.ascontiguousarray(np.stack(out), dtype=np.float32)


def _shared_weights(inp):
    ar = np.arange
    sh = {}
    sh["wmod"] = np.stack([_tile_cols(inp["w_mod"][l], [ar(j * 512, (j + 1) * 512) for j in range(12)]) for l in range(4)])
    fup = []
    for l in range(4):
        cl = []
        for pg in range(11):
            cl.append(np.concatenate([ar(pg * 256, (pg + 1) * 256), FFN + ar(pg * 256, (pg + 1) * 256)]))
        fup.append(_tile_cols(inp["ffn_w_up"][l], cl))
    sh["fup"] = np.stack(fup)
    sh["fdn"] = np.stack([_tile_cols(inp["ffn_w_down"][l], [ar(m * 256, (m + 1) * 256) for m in range(4)]) for l in range(4)])
    sh["ssdin"] = np.stack([_tile_cols(inp["ssd_w_in"][j], [ar(i * 512, (i + 1) * 512) for i in range(12)]) for j in range(2)])
    sh["ssddt"] = np.stack([_tile_cols(inp["ssd_w_in"][j], [ar(6144, 6208)])[0] for j in range(2)])
    sh["ssdout"] = np.stack([_tile_cols(inp["ssd_w_out"][j], [ar(m * 256, (m + 1) * 256) for m in range(4)]) for j in range(2)])
    sh["scin"] = _tile_cols(inp["sc_w_in"][0], [np.concatenate([ar(i * 128, (i + 1) * 128), 1024 + ar(i * 128, (i + 1) * 128), 2048 + ar(i * 128, (i + 1) * 128)]) for i in range(8)])
    sh["scout"] = _tile_cols(inp["sc_w_out"][0], [ar(0, 1024)])[0]
    sh["qkv"] = _tile_cols(inp["da_w_qkv"][0], [np.concatenate([ar(i * 128, (i + 1) * 128), 1024 + ar(i * 128, (i + 1) * 128), 2048 + ar(i * 128, (i + 1) * 128)]) for i in range(8)])
    sh["daout"] = _tile_cols(inp["da_w_out"][0], [ar(0, 1024)])[0]
    return sh


def _const_mats():
    k = np.arange(128)[:, None]
    s = np.arange(128)[None, :]
    m = {}
    m["ident"] = (k == s).astype(np.float32)
    m["ones"] = np.ones((128, 128), np.float32)
    P = np.zeros((128, 128), np.float32)
    for hb_ in (0, 64):
        for i in range(32):
            P[hb_ + i + 32, hb_ + i] = -1.0
            P[hb_ + i, hb_ + i + 32] = 1.0
    m["rotp"] = P
    m["triU"] = (k <= s).astype(np.float32)
    m["triL"] = (k >= s).astype(np.float32)
    m["LF"] = (k > s).astype(np.float32)
    m["LB"] = (k < s).astype(np.float32)
    m["maskF"] = np.where(k <= s, 0.0, NEG).astype(np.float32)
    m["maskB"] = np.where(k >= s, 0.0, NEG).astype(np.float32)
    return m


def _consts(inp, cond, is_sample):
    C = np.zeros((128, NCONST), np.float32)

    def put(name, arr):
        off, n = CMAP[name]
        arr = np.asarray(arr, np.float32).reshape(128, n)
        C[:, off:off + n] = arr

    put("cond", _fm(cond, 8))
    put("flag", np.full((128, 1), 1.0 if is_sample else 0.0))
    put("flagm1", np.full((128, 1), 0.0 if is_sample else -1.0))
    put("bmod", np.concatenate([_fm(inp["b_mod"][l], 48) for l in range(4)], axis=1))
    put("normg", np.concatenate([_fm(inp["norm_g"][l, i], 8) for l in range(4) for i in range(4)], axis=1))
    cw = np.zeros((128, 3, NCW), np.float32)
    for tap in range(3):
        for l in range(4):
            cw[:, tap, CW_FFN + l * 44: CW_FFN + (l + 1) * 44] = _fm(inp["ffn_conv_w"][l, tap], 44)
        for j in range(2):
            cw[:, tap, CW_SSD + j * 32: CW_SSD + (j + 1) * 32] = _fm(inp["ssd_conv_w"][j, tap], 32)
        cw[:, tap, CW_SC: CW_SC + 8] = _fm(inp["sc_conv_w"][0, tap], 8)
    put("cw", cw)
    put("ssd_cb", np.concatenate([_fm(inp["ssd_conv_b"][j], 32) for j in range(2)], axis=1))
    dtb = np.zeros((128, 2), np.float32)
    alog = np.zeros((128, 2), np.float32)
    for j in range(2):
        dtb[:64, j] = np.asarray(inp["ssd_dt_bias"][j]).reshape(64)
        alog[:64, j] = np.asarray(inp["ssd_a_log"][j]).reshape(64)
    put("ssd_dtb", dtb)
    put("ssd_alog", alog)
    put("ssd_d", np.concatenate([_fm(np.repeat(np.asarray(inp["ssd_d"][j]), 64), 16) for j in range(2)], axis=1))
    put("ssd_ng", np.concatenate([_fm(inp["ssd_norm_g"][j], 16) for j in range(2)], axis=1))
    put("lam", np.broadcast_to(np.asarray(inp["da_lambda"][0]).reshape(1, 256), (128, 256)))
    put("subg", np.asarray(inp["da_subln_g"][0]).reshape(128, 1))
    ab = np.zeros((18, 8), np.float32)
    if not is_sample:
        ab[:] = NEG
        for kb in range(2, 18):
            ab[kb, (kb - 2) // 2] = 0.0
    put("abias", np.broadcast_to(ab.reshape(1, 144), (128, 144)))
    for k_, v_ in _const_mats().items():
        put(k_, v_)
    return C


def _rot_tables(is_sample):
    R = np.zeros((128, 2, T), np.float32)
    if not is_sample:
        R[:, 0, :] = 1.0
        return R
    rows = T // 64
    row = np.repeat(np.arange(rows, dtype=np.float32), 64)
    col = np.tile(np.arange(64, dtype=np.float32), rows)
    inv = (10000.0 ** (-np.arange(16, dtype=np.float32) / 16)).astype(np.float32)
    ang = np.concatenate([row[:, None] * inv, col[:, None] * inv], axis=-1).astype(np.float32)
    cos, sin = np.cos(ang).T, np.sin(ang).T
    for p in range(128):
        R[p, 0] = cos[(p % 64) % 32]
        R[p, 1] = sin[(p % 64) % 32]
    return R


def _core_inputs(inp, core, shared):
    is_sample = core < 4
    if is_sample:
        xs = np.asarray(inp["x_sample"][core], np.float32)
        cond = np.asarray(inp["c"][core])
        st = np.asarray(inp["state_ssm"][core], np.float32)
        ssm0 = np.ascontiguousarray(st.transpose(0, 1, 4, 2, 3).reshape(2, 2, 128, 2048))
        kc_ = np.asarray(inp["cache_k"][core, 0], np.float32).reshape(256, 8, 128)
        kcache = np.ascontiguousarray(kc_.transpose(2, 1, 0))
        vc_ = np.asarray(inp["cache_v"][core, 0], np.float32).reshape(2, 128, 1024)
        vcache = np.ascontiguousarray(vc_.transpose(1, 0, 2))
    else:
        s0 = (core - 4) * 8
        xs = np.asarray(inp["x_prompt"][s0:s0 + 8], np.float32).reshape(T, D)
        cond = np.asarray(inp["c_ctx"])
        ssm0 = np.zeros((2, 2, 128, 2048), np.float32)
        kcache = np.zeros((128, 8, 256), np.float32)
        vcache = np.zeros((128, 2, 1024), np.float32)
    d = dict(shared)
    d["xT"] = np.ascontiguousarray(xs.reshape(T, KC, 128).transpose(2, 1, 0))
    d["consts"] = _consts(inp, cond, is_sample)
    d["rot"] = _rot_tables(is_sample)
    d["ssm0"] = ssm0
    d["kcache"] = kcache
    d["vcache"] = vcache
    return d


_BUILD_CACHE = {}
_DEBUG = [False]


def _get_builder(layers, mixers=None):
    key = (tuple(layers), None if mixers is None else tuple(mixers))
    if key not in _BUILD_CACHE:
        b = Builder(layers, mixers)
        b.debug = _DEBUG[0]
        b.build()
        _BUILD_CACHE[key] = b
    return _BUILD_CACHE[key]


def run_cores(inputs, layers=(0, 1, 2, 3), mixers=None):
    inp = {k: np.asarray(v) for k, v in inputs.items()}
    shared = _shared_weights(inp)
    in_maps = [_core_inputs(inp, c, shared) for c in range(8)]
    b = _get_builder(layers, mixers)
    res = run_bass_kernel_spmd(b.nc, in_maps, core_ids=list(range(8)))
    return res.results


def kernel(**inputs):
    r = run_cores(inputs)
    B, SEQ = 32, 256
    y_s = np.stack([r[c]["yT"].transpose(2, 1, 0).reshape(T, D) for c in range(4)]).astype(np.float32)
    y_p = np.concatenate([r[c]["yT"].transpose(2, 1, 0).reshape(8, SEQ, D) for c in range(4, 8)]).astype(np.float32)
    st = np.concatenate([r[c]["stout"].reshape(2, 8, 2, 128, 32, 64).transpose(1, 0, 2, 4, 5, 3) for c in range(4, 8)]).astype(np.float32)
    kn = np.concatenate([r[c]["knew"].transpose(2, 1, 0).reshape(8, 1, SEQ, 16, 64) for c in range(4, 8)]).astype(np.float32)
    vn = np.concatenate([r[c]["vnew"].transpose(1, 0, 2).reshape(8, 1, SEQ, 8, 128) for c in range(4, 8)]).astype(np.float32)
    return (np.ascontiguousarray(y_p), np.ascontiguousarray(y_s), np.ascontiguousarray(st), np.ascontiguousarray(kn), np.ascontiguousarray(vn))
```
